# Optimizing a Trainium2 kernel written in Bass

```python
import math
import jax, jax.numpy as jnp
from jax import lax
import numpy as np

D_MODEL = 2048
BATCH = 2
SEQ = 4096
DEPTH = 4
DEC_BATCH = 32
DEC_SEQ = 8
PAST_LEN = 16384
PAGE_SIZE = 128

N_A_LAYERS = DEPTH // 2
N_B_LAYERS = DEPTH - N_A_LAYERS
HGRN_HEADS = 16
HGRN_DK = D_MODEL // HGRN_HEADS
HGRN_DV = D_MODEL // HGRN_HEADS
HGRN_CHUNK = 16
ATTN_HEAD_DIM = 64
ATTN_Q_HEADS = D_MODEL // ATTN_HEAD_DIM
ATTN_KV_HEADS = 8
GQA_GROUP = ATTN_Q_HEADS // ATTN_KV_HEADS
WINDOW = 128
ROPE_THETA = 10000.0
ATTN_SCALE = ATTN_HEAD_DIM ** -0.5
D_FF = -(-8 * D_MODEL // (3 * 256)) * 256
RMS_EPS = 1e-6
MASK_VALUE = -1e30
MIN_FORGET = 1e-30

kernel_name = 'hybrid_hgrn2_yoco_swa_sink_step'


def rms_norm(x, g):
    xf = x.astype(jnp.float32)
    y = xf * lax.rsqrt(jnp.mean(xf * xf, axis=-1, keepdims=True) + RMS_EPS)
    return (y * g.astype(jnp.float32)).astype(x.dtype)


def rope(x, pos):
    half = x.shape[-1] // 2
    inv = ROPE_THETA ** (-jnp.arange(half, dtype=jnp.float32) / half)
    ang = pos.astype(jnp.float32)[:, None] * inv[None, :]
    cos = jnp.cos(ang)[:, None, :]
    sin = jnp.sin(ang)[:, None, :]
    xf = x.astype(jnp.float32)
    x1, x2 = xf[..., :half], xf[..., half:]
    return jnp.concatenate([x1 * cos - x2 * sin, x2 * cos + x1 * sin], axis=-1).astype(x.dtype)


def swiglu(h, w_gate, w_up, w_down):
    return (jax.nn.silu(h @ w_gate) * (h @ w_up)) @ w_down


def gla_chunked(q, k, log_f, v, s0, chunk):
    B, T, H, DK = q.shape
    n = T // chunk

    def to_chunks(a):
        return jnp.moveaxis(a.reshape(B, n, chunk, H, a.shape[-1]), 1, 0)

    causal = jnp.tril(jnp.ones((chunk, chunk), dtype=bool))[None, :, :, None, None]

    def step(S, inp):
        qc, kc, gc, vc = inp
        G = jnp.cumsum(gc, axis=1)
        diff = G[:, :, None] - G[:, None, :]
        decay = jnp.where(causal, jnp.exp(jnp.minimum(diff, 0.0)), 0.0)
        A = jnp.einsum('bihd,bjhd,bijhd->bhij', qc, kc, decay)
        o_intra = jnp.einsum('bhij,bjhv->bihv', A, vc)
        o_inter = jnp.einsum('bihd,bhdv->bihv', qc * jnp.exp(G), S)
        G_last = G[:, -1]
        k_dec = kc * jnp.exp(G_last[:, None] - G)
        S_new = jnp.exp(G_last)[..., None] * S + jnp.einsum('bjhd,bjhv->bhdv', k_dec, vc)
        return S_new, o_intra + o_inter

    s_fin, o = lax.scan(step, s0, (to_chunks(q), to_chunks(k), to_chunks(log_f), to_chunks(v)))
    o = jnp.moveaxis(o, 0, 1).reshape(B, T, H, v.shape[-1])
    return o, s_fin


def hgrn2_mixer(h, w_q, w_f, w_i, w_g, lb, g_o, w_o, s0, chunk):
    B, T, _ = h.shape
    shp = (B, T, HGRN_HEADS, HGRN_DK)
    q = jax.nn.silu(h @ w_q).reshape(shp).astype(jnp.float32) * (HGRN_DK ** -0.5)
    z = (h @ w_f).reshape(shp).astype(jnp.float32)
    lb = lb.reshape(HGRN_HEADS, HGRN_DK)
    f = lb + (1.0 - lb) * jax.nn.sigmoid(z)
    log_f = jnp.log(jnp.maximum(f, MIN_FORGET))
    k = (1.0 - lb) * jax.nn.sigmoid(-z)
    v = (h @ w_i).reshape(B, T, HGRN_HEADS, HGRN_DV).astype(jnp.float32)
    o, s_fin = gla_chunked(q, k, log_f, v, s0.astype(jnp.float32), chunk)
    o = rms_norm(o, g_o).reshape(B, T, D_MODEL).astype(h.dtype)
    o = o * jax.nn.silu(h @ w_g)
    return o @ w_o, s_fin


def sink_probs(s, mask, sinks):
    s = jnp.where(mask, s, MASK_VALUE)
    sk = sinks.astype(jnp.float32).reshape(ATTN_KV_HEADS, GQA_GROUP, 1, 1)
    m = jnp.maximum(jnp.max(s, axis=-1, keepdims=True), sk)
    p = jnp.where(mask, jnp.exp(s - m), 0.0)
    return p / (jnp.sum(p, axis=-1, keepdims=True) + jnp.exp(sk - m))


def swa_prompt(q, k, v, sinks):
    B, T = q.shape[:2]
    n = T // WINDOW
    qb = q.reshape(B, n, WINDOW, ATTN_KV_HEADS, GQA_GROUP, ATTN_HEAD_DIM)
    pad = ((0, 0), (WINDOW, 0), (0, 0), (0, 0))
    kb = jnp.pad(k, pad).reshape(B, n + 1, WINDOW, ATTN_KV_HEADS, ATTN_HEAD_DIM)
    vb = jnp.pad(v, pad).reshape(B, n + 1, WINDOW, ATTN_KV_HEADS, ATTN_HEAD_DIM)
    kband = jnp.concatenate([kb[:, :-1], kb[:, 1:]], axis=2)
    vband = jnp.concatenate([vb[:, :-1], vb[:, 1:]], axis=2)
    s = jnp.einsum('bnqhgd,bnkhd->bnhgqk', qb, kband).astype(jnp.float32) * ATTN_SCALE
    i = jnp.arange(WINDOW)[:, None]
    j = jnp.arange(2 * WINDOW)[None, :]
    rel = WINDOW + i - j
    blk = jnp.arange(n)[:, None, None]
    mask = (rel >= 0) & (rel < WINDOW) & ((blk - 1) * WINDOW + j >= 0)
    p = sink_probs(s, mask[None, :, None, None], sinks)
    o = jnp.einsum('bnhgqk,bnkhd->bnqhgd', p.astype(v.dtype), vband)
    return o.reshape(B, T, ATTN_Q_HEADS * ATTN_HEAD_DIM)


def swa_sample(q, k_all, v_all, sinks):
    B, S = q.shape[:2]
    L = k_all.shape[1]
    qg = q.reshape(B, S, ATTN_KV_HEADS, GQA_GROUP, ATTN_HEAD_DIM)
    s = jnp.einsum('bqhgd,bkhd->bhgqk', qg, k_all).astype(jnp.float32) * ATTN_SCALE
    rel = (L - S) + jnp.arange(S)[:, None] - jnp.arange(L)[None, :]
    mask = (rel >= 0) & (rel < WINDOW)
    p = sink_probs(s, mask, sinks)
    o = jnp.einsum('bhgqk,bkhd->bqhgd', p.astype(v_all.dtype), v_all)
    return o.reshape(B, S, ATTN_Q_HEADS * ATTN_HEAD_DIM)


def trunk(x, pos, s0, chunk, past_k, past_v,
          hgrn_norm, hgrn_wq, hgrn_wf, hgrn_wi, hgrn_wg, lower_bounds, hgrn_onorm, hgrn_wo,
          kv_norm, w_k, w_v, k_norm,
          attn_norm, attn_wq, q_norm, sinks, attn_wo,
          ffn_norm, w_gate, w_up, w_down):
    B, T, _ = x.shape
    new_states = []
    k_all = None
    v_all = None
    for l in range(DEPTH):
        if l < N_A_LAYERS:
            h = rms_norm(x, hgrn_norm[l])
            o, s_fin = hgrn2_mixer(h, hgrn_wq[l], hgrn_wf[l], hgrn_wi[l], hgrn_wg[l],
                                   lower_bounds[l], hgrn_onorm[l], hgrn_wo[l], s0[l], chunk)
            new_states.append(s_fin)
        else:
            j = l - N_A_LAYERS
            if j == 0:
                hkv = rms_norm(x, kv_norm)
                k_new = rope(rms_norm((hkv @ w_k).reshape(B, T, ATTN_KV_HEADS, ATTN_HEAD_DIM), k_norm), pos)
                v_new = (hkv @ w_v).reshape(B, T, ATTN_KV_HEADS, ATTN_HEAD_DIM)
                if past_k is None:
                    k_all, v_all = k_new, v_new
                else:
                    k_all = jnp.concatenate([past_k.astype(k_new.dtype), k_new], axis=1)
                    v_all = jnp.concatenate([past_v.astype(v_new.dtype), v_new], axis=1)
            h = rms_norm(x, attn_norm[j])
            q = rope(rms_norm((h @ attn_wq[j]).reshape(B, T, ATTN_Q_HEADS, ATTN_HEAD_DIM), q_norm[j]), pos)
            if past_k is None:
                a = swa_prompt(q, k_all, v_all, sinks[j])
            else:
                a = swa_sample(q, k_all, v_all, sinks[j])
            o = a @ attn_wo[j]
        x = x + o
        x = x + swiglu(rms_norm(x, ffn_norm[l]), w_gate[l], w_up[l], w_down[l])
    return x, jnp.stack(new_states), k_all[:, -WINDOW:], v_all[:, -WINDOW:]


def setup_inputs(seed: int = 0) -> dict:
    key = jax.random.key(seed)
    ks = jax.random.split(key, 32)
    f32 = jnp.float32

    def nrm(k, shape, scale):
        return jax.random.normal(k, shape, f32) * scale

    def gain(k, shape):
        return 1.0 + 0.05 * jax.random.normal(k, shape, f32)

    D = D_MODEL
    KVW = ATTN_KV_HEADS * ATTN_HEAD_DIM
    QW = ATTN_Q_HEADS * ATTN_HEAD_DIM
    return {
        'x_prompt': nrm(ks[0], (BATCH, SEQ, D), 1.0),
        'x_sample': nrm(ks[1], (DEC_BATCH, DEC_SEQ, D), 1.0),
        'state_hgrn': nrm(ks[2], (N_A_LAYERS, DEC_BATCH, HGRN_HEADS, HGRN_DK, HGRN_DV), 0.5),
        'cache_k_win': nrm(ks[3], (DEC_BATCH, WINDOW, ATTN_KV_HEADS, ATTN_HEAD_DIM), 1.0),
        'cache_v_win': nrm(ks[4], (DEC_BATCH, WINDOW, ATTN_KV_HEADS, ATTN_HEAD_DIM), 1.0),
        'hgrn_norm': gain(ks[5], (N_A_LAYERS, D)),
        'hgrn_wq': nrm(ks[6], (N_A_LAYERS, D, HGRN_HEADS * HGRN_DK), D ** -0.5),
        'hgrn_wf': nrm(ks[7], (N_A_LAYERS, D, HGRN_HEADS * HGRN_DK), D ** -0.5),
        'hgrn_wi': nrm(ks[8], (N_A_LAYERS, D, HGRN_HEADS * HGRN_DV), D ** -0.5),
        'hgrn_wg': nrm(ks[9], (N_A_LAYERS, D, D), D ** -0.5),
        'hgrn_lb_logits': nrm(ks[10], (N_A_LAYERS, HGRN_HEADS * HGRN_DK), 0.5),
        'hgrn_onorm': gain(ks[11], (N_A_LAYERS, HGRN_DV)),
        'hgrn_wo': nrm(ks[12], (N_A_LAYERS, D, D), D ** -0.5),
        'kv_norm': gain(ks[13], (D,)),
        'w_k': nrm(ks[14], (D, KVW), D ** -0.5),
        'w_v': nrm(ks[15], (D, KVW), D ** -0.5),
        'k_norm': gain(ks[16], (ATTN_HEAD_DIM,)),
        'attn_norm': gain(ks[17], (N_B_LAYERS, D)),
        'attn_wq': nrm(ks[18], (N_B_LAYERS, D, QW), D ** -0.5),
        'q_norm': gain(ks[19], (N_B_LAYERS, ATTN_HEAD_DIM)),
        'sinks': nrm(ks[20], (N_B_LAYERS, ATTN_Q_HEADS), 0.5),
        'attn_wo': nrm(ks[21], (N_B_LAYERS, QW, D), QW ** -0.5),
        'ffn_norm': gain(ks[22], (DEPTH, D)),
        'w_gate': nrm(ks[23], (DEPTH, D, D_FF), D ** -0.5),
        'w_up': nrm(ks[24], (DEPTH, D, D_FF), D ** -0.5),
        'w_down': nrm(ks[25], (DEPTH, D_FF, D), D_FF ** -0.5),
    }


def reference(x_prompt, x_sample, state_hgrn, cache_k_win, cache_v_win,
              hgrn_norm, hgrn_wq, hgrn_wf, hgrn_wi, hgrn_wg, hgrn_lb_logits, hgrn_onorm, hgrn_wo,
              kv_norm, w_k, w_v, k_norm,
              attn_norm, attn_wq, q_norm, sinks, attn_wo,
              ffn_norm, w_gate, w_up, w_down):
    sm = jax.nn.softmax(hgrn_lb_logits.astype(jnp.float32), axis=0)
    lower_bounds = jnp.cumsum(sm, axis=0) - sm[0:1]
    weights = (hgrn_norm, hgrn_wq, hgrn_wf, hgrn_wi, hgrn_wg, lower_bounds, hgrn_onorm, hgrn_wo,
               kv_norm, w_k, w_v, k_norm,
               attn_norm, attn_wq, q_norm, sinks, attn_wo,
               ffn_norm, w_gate, w_up, w_down)
    s0_prompt = jnp.zeros((N_A_LAYERS, x_prompt.shape[0], HGRN_HEADS, HGRN_DK, HGRN_DV), jnp.float32)
    y_prompt, st_prompt, kw_prompt, vw_prompt = trunk(
        x_prompt, jnp.arange(x_prompt.shape[1]), s0_prompt, HGRN_CHUNK, None, None, *weights)
    n_new = x_sample.shape[1]
    y_sample, st_sample, kw_sample, vw_sample = trunk(
        x_sample, PAST_LEN + jnp.arange(n_new), state_hgrn, n_new, cache_k_win, cache_v_win, *weights)
    return (y_prompt, y_sample, st_prompt, st_sample, kw_prompt, vw_prompt, kw_sample, vw_sample)
```

```python
import numpy as np
from contextlib import ExitStack
import concourse.bass as bass
import concourse.mybir as mybir
from concourse.bass_utils import run_bass_kernel_spmd

F32, BF16 = mybir.dt.float32, mybir.dt.bfloat16
AF = mybir.ActivationFunctionType
ALU = mybir.AluOpType

D = 2048
NKC = 16
DFF = 5632
NF = 44
NFC = 11
SEQ = 4096
TP = 1024
NS = 4
TS = 8
TW = TP + NS * TS
NT = 4
PAST = 16384
EPS = 1e-6
LN_MINF = float(np.log(1e-30))
QSC = 128 ** -0.5
ASC = 64 ** -0.5
NEG = -1e30


class Sched:
    def __init__(self, nc, es):
        self.nc = nc
        self.es = es
        self.E = {}
        for n in ("pe", "act", "dve", "sp", "gq"):
            self.E[n] = dict(name=n, recs=[], seen={}, dma=n in ("sp", "gq"), pend=[])
        for n in ("pe", "act", "dve"):
            e = self.E[n]
            e["sem"] = es.enter_context(nc.semaphore(n + "_s0"))
            e["cnt"] = 0
            e["ep"] = 0
        for n in ("sp", "gq"):
            e = self.E[n]
            e["sems"] = [es.enter_context(nc.semaphore("%s_d%d" % (n, i))) for i in range(8)]
            e["tot"] = [0] * 8
            e["n"] = 0
        self.bw = {}
        self.br = {}
        self.last = {}

    def _deps(self, eng, reads, writes):
        toks = []
        for k in reads:
            if k in self.bw:
                toks.append(self.bw[k])
        for k in writes:
            if k in self.bw:
                toks.append(self.bw[k])
            toks.extend(self.br.get(k, ()))
        need = {}
        for (sem, val, src) in toks:
            if src == "pe" and eng["name"] == "pe":
                continue
            key = id(sem)
            if key not in need or need[key][1] < val:
                need[key] = (sem, val)
        waits = list(eng["pend"])
        eng["pend"] = []
        for key, (sem, val) in need.items():
            if eng["seen"].get(key, 0) < val:
                eng["seen"][key] = val
                waits.append((sem, val))
        return waits

    def op(self, en, fn, reads=(), writes=(), inc=True):
        eng = self.E[en]
        waits = self._deps(eng, reads, writes)
        if eng["dma"]:
            i = eng["n"] % len(eng["sems"])
            eng["n"] += 1
            sem = eng["sems"][i]
            if eng["tot"][i] > 0 and eng["seen"].get(id(sem), 0) < eng["tot"][i]:
                eng["seen"][id(sem)] = eng["tot"][i]
                waits.append((sem, eng["tot"][i]))
            eng["tot"][i] += 16
            tok = (sem, eng["tot"][i], en)
            eng["recs"].append((waits, fn, sem, 16))
        elif inc:
            if eng["cnt"] >= 30000:
                eng["ep"] += 1
                eng["sem"] = self.es.enter_context(self.nc.semaphore("%s_s%d" % (en, eng["ep"])))
                eng["cnt"] = 0
            eng["cnt"] += 1
            tok = (eng["sem"], eng["cnt"], en)
            eng["recs"].append((waits, fn, eng["sem"], 1))
        else:
            eng["recs"].append((waits, fn, None, 0))
            return None
        self.last[en + str(id(tok[0]))] = tok
        for k in reads:
            self.br.setdefault(k, []).append(tok)
        for k in writes:
            self.bw[k] = tok
            self.br[k] = []
        return tok

    def pe(self, fns, reads, writes):
        for f in fns[:-1]:
            self.op("pe", f, reads, writes, inc=False)
        return self.op("pe", fns[-1], reads, writes, inc=True)

    def barrier(self):
        allw = [(t[0], t[1]) for t in self.last.values()]
        for e in self.E.values():
            e["pend"] = list(allw)
            for (sem, val) in allw:
                e["seen"][id(sem)] = max(e["seen"].get(id(sem), 0), val)

    def finish(self):
        eng = self.E["sp"]
        waits = []
        for tok in self.last.values():
            waits.append((tok[0], tok[1]))
        eng["recs"].append((waits, None, None, 0))

    def emit(self, block):
        def replay(recs):
            def run(e):
                for (waits, fn, sem, amt) in recs:
                    for (s, v) in waits:
                        e.wait_ge(s, v)
                    if fn is not None:
                        ins = fn(e)
                        if sem is not None:
                            ins.then_inc(sem, amt)
            return run
        block.sync(replay(self.E["sp"]["recs"]))
        block.gpsimd(replay(self.E["gq"]["recs"]))
        block.tensor(replay(self.E["pe"]["recs"]))
        block.scalar(replay(self.E["act"]["recs"]))
        block.vector(replay(self.E["dve"]["recs"]))


def build_nc(cfg=None):
    cfg = cfg or {}
    n_tiles = cfg.get("n_tiles", NT)
    do_attn = cfg.get("attn", True)
    nc = bass.Bass("TRN2", target_bir_lowering=False, dynamic_dma_scratch_size=4096)
    es = ExitStack()
    with es:
        def din(name, shape):
            return nc.dram_tensor(name, list(shape), F32, kind="ExternalInput").ap()

        def dout(name, shape):
            return nc.dram_tensor(name, list(shape), F32, kind="ExternalOutput").ap()

        def sb(name, shape, dt):
            return es.enter_context(nc.sbuf_tensor(name, list(shape), dt))

        def ps(name, shape, dt):
            return es.enter_context(nc.psum_tensor(name, list(shape), dt))

        xin = din("xin", [NT, 128, NKC * TP])
        xs_d = din("xs", [128, NKC * NS * TS])
        st_in = din("st_in", [2, NS, 16, 128, 128])
        ck_d = din("ck", [NS, 128, 512])
        cv_d = din("cv", [NS, 128, 512])
        hw_d = {k: din("hw_" + k, [2, 16, 128, 2048]) for k in ("q", "f", "i", "g", "o")}
        wgate_d = din("wgate", [4, NF, 128, 2048])
        wup_d = din("wup", [4, NF, 128, 2048])
        wdown_d = din("wdown", [4, NFC, 4, 128, 2048])
        awq_d = din("awq", [2, 16, 128, 2048])
        awo_d = din("awo", [2, 16, 128, 2048])
        wk_d = din("wk", [4, 128, 2048])
        wv_d = din("wv", [4, 128, 2048])
        cols_d = din("cols", [128, 320])
        rope_d = din("rope", [4, 128, TW])
        mask_d = din("masks", [4, 128, 128])
        cst_d = din("cst", [4, 128, 128])
        scm_d = din("scm", [128, 544])
        lbl_d = din("lbl", [128, 32])

        yT_d = dout("yT", [128, NKC * TW])
        stp_d = dout("stp", [2, 16, 128, 128])
        sts_d = dout("sts", [2, NS, 16, 128, 128])
        kwp_d = dout("kwp", [4, 128, 128])
        vwp_d = dout("vwp", [4, 128, 128])
        kws_old = dout("kws_old", [NS, 120, 512])
        vws_old = dout("vws_old", [NS, 120, 512])
        kws_new = dout("kws_new", [4, 128, NS * TS])
        vws_new = dout("vws_new", [4, 128, NS * TS])

        xT = sb("xT", [128, NKC, TW], F32)
        hT = sb("hT", [128, NKC, TW], BF16)
        ogT = sb("ogT", [128, NKC * TW], BF16)
        arena = sb("arena", [128, 8272], F32)
        Sst = arena[:, 0:4096].rearrange("p (a b c) -> p a b c", b=16, c=128)
        khalo = arena[:, 7760:8016].bitcast(BF16).rearrange("p (a b) -> p a b", b=128)
        vhalo = arena[:, 8016:8272].bitcast(BF16).rearrange("p (a b) -> p a b", b=128)
        wr = sb("wr", [128, 6, 2048], BF16)
        t32 = sb("t32", [128, 6, 512], F32)
        t16 = sb("t16", [128, 4, 512], BF16)
        sqb = sb("sqb", [128, 1, 512], BF16)
        rstd = sb("rstd", [128, 512], F32)
        kvtm = sb("kvtm", [32, 2, 2, 128], BF16)
        A_sb = sb("A_sb", [32, 2, 32], BF16)
        S_bf = sb("S_bf", [128, 2, 128], BF16)
        Ssmp = sb("Ssmp", [128, 2, 128], F32)
        Ssmo = sb("Ssmo", [128, 2, 128], F32)
        Stmp = sb("Stmp", [128, 128], F32)
        colsT = sb("colsT", [128, 320], F32)
        ecol = sb("ecol", [128, 8], F32)
        ecol_l = sb("ecol_l", [128, 32], F32)
        cst = sb("cst_sb", [128, 4, 128], F32)
        cstb = sb("cstb", [128, 2, 128], BF16)
        msk = sb("msk", [128, 4, 128], F32)
        scm = sb("scm_sb", [128, 544], F32)

        pP = ps("pP", [128, 2, 512], F32)
        pO = ps("pO", [128, 512], F32)
        pM = ps("pM", [128, 512], F32)
        pTb = ps("pTb", [128, 1024], BF16)
        pN = ps("pN", [128, 512], F32)
        pU = ps("pU", [128, 2, 512], F32)

        S = Sched(nc, es)
        ident_b = cstb[:, 0, :]
        ones_b = cstb[:, 1, :]
        ident_f = cst[:, 0, :]
        bd64 = cst[:, 2, :]
        rotP = cst[:, 3, :]
        tri = msk[0:32, 3, 0:32]

        C_HN, C_AN, C_FN, C_KVN = 0, 32, 64, 128
        C_LB, C_OMLB, C_NOMLB = 144, 176, 208
        C_GO, C_KN, C_QN, C_SK = 240, 242, 243, 256

        S.op("sp", lambda e: e.dma_start(out=colsT[:], in_=cols_d), writes=["colsT"])
        S.op("sp", lambda e: e.dma_start(out=cst[:], in_=cst_d.rearrange("a p c -> p a c")), writes=["cst"])
        S.op("sp", lambda e: e.dma_start(out=msk[:], in_=mask_d.rearrange("a p c -> p a c")), writes=["msk"])
        S.op("sp", lambda e: e.dma_start(out=scm[:], in_=scm_d), writes=["scm"])
        S.op("act", lambda e: e.activation(out=cstb[:], in_=cst[:, 0:2, :], func=AF.Copy), reads=["cst"], writes=["cstb"])
        S.op("sp", lambda e: e.dma_start(out=ecol_l[:], in_=lbl_d), writes=["ecol_l"])
        S.op("dve", lambda e: e.tensor_tensor(out=ecol_l[:, 0:16], in0=ecol_l[:, 16:32], in1=ecol_l[:, 0:16], op=ALU.subtract),
             reads=["ecol_l"], writes=["ecol_l"])
        S.op("act", lambda e: e.activation(out=colsT[:, C_LB + 16:C_LB + 32], in_=ecol_l[:, 0:16], func=AF.Sigmoid),
             reads=["ecol_l", "colsT"], writes=["colsT"])
        S.op("dve", lambda e: e.tensor_scalar(out=colsT[:, C_OMLB:C_OMLB + 32], in0=colsT[:, C_LB:C_LB + 32], scalar1=-1.0, scalar2=1.0,
                                              op0=ALU.mult, op1=ALU.add), reads=["colsT"], writes=["colsT"])
        S.op("dve", lambda e: e.tensor_scalar(out=colsT[:, C_NOMLB:C_NOMLB + 32], in0=colsT[:, C_OMLB:C_OMLB + 32], scalar1=-1.0, scalar2=None,
                                              op0=ALU.mult), reads=["colsT"], writes=["colsT"])

        groups_main = [(0, 512), (512, 512)]
        wslot = [0]

        def load_w(dram_ap, nslots=1):
            s0 = wslot[0]
            if s0 + nslots > 6:
                s0 = 0
            wslot[0] = (s0 + nslots) % 6
            keys = ["w%d" % (s0 + i) for i in range(nslots)]
            for i in range(nslots):
                src = dram_ap[i] if nslots > 1 else dram_ap
                S.op("gq", (lambda e, i=i, src=src: e.dma_start(out=wr[:, s0 + i, :], in_=src)), writes=[keys[i]])
            return s0, keys

        def proj_fns(pout, wslot_idx, src, c0, n):
            return [(lambda e, kc=kc: e.matmul(pout, lhsT=wr[:, wslot_idx, kc * 128:(kc + 1) * 128],
                                               rhs=src[:, kc, c0:c0 + n], start=(kc == 0), stop=(kc == NKC - 1)))
                    for kc in range(NKC)]

        def proj(pout, wslot_idx, wkeys, src, srckeys, c0, n, outkey):
            fns = []
            for kc in range(NKC):
                fns.append(lambda e, kc=kc: e.matmul(pout, lhsT=wr[:, wslot_idx, kc * 128:(kc + 1) * 128],
                                                     rhs=src[:, kc, c0:c0 + n], start=(kc == 0), stop=(kc == NKC - 1)))
            S.pe(fns, reads=list(wkeys) + list(srckeys), writes=[outkey])

        def rstd_from(psum_ap, n, inv_n, pkey):
            S.op("act", lambda e: e.activation(out=rstd[:, 0:n], in_=psum_ap, func=AF.Ln, scale=inv_n, bias=ecol[:, 0:1]),
                 reads=[pkey, "ecol"], writes=["rstd"])
            S.op("act", lambda e: e.activation(out=rstd[:, 0:n], in_=rstd[:, 0:n], func=AF.Exp, scale=-0.5),
                 reads=["rstd"], writes=["rstd"])

        S.op("dve", lambda e: e.memset(ecol[:, 0:1], EPS), writes=["ecol"])

        def rmsnorm_to_h(gcol0, groups, srcT=None):
            for (c0, n) in groups:
                _rms_group(gcol0, c0, n)

        def _rms_group(gcol0, c0, n):
            if True:
                fns = []
                for kc in range(NKC):
                    b = kc % 4
                    S.op("act", (lambda e, kc=kc, b=b: e.activation(out=t16[:, b, 0:n], in_=xT[:, kc, c0:c0 + n], func=AF.Square)),
                         reads=["xT"], writes=["t16_%d" % b])
                    S.op("pe", (lambda e, kc=kc, b=b: e.matmul(pN[:, 0:n], lhsT=ones_b, rhs=t16[:, b, 0:n],
                                                               start=(kc == 0), stop=(kc == NKC - 1))),
                         reads=["t16_%d" % b, "cstb"], writes=["pN"], inc=True)
                rstd_from(pN[:, 0:n], n, 1.0 / D, "pN")
                for kc in range(NKC):
                    S.op("dve", (lambda e, kc=kc: e.scalar_tensor_tensor(out=hT[:, kc, c0:c0 + n], in0=xT[:, kc, c0:c0 + n],
                                                                         scalar=colsT[:, gcol0 + kc:gcol0 + kc + 1], in1=rstd[:, 0:n],
                                                                         op0=ALU.mult, op1=ALU.mult)),
                         reads=["xT", "rstd", "colsT"], writes=["hT"])

        def hgrn_layer(l, groups, last_tile):
            rmsnorm_to_h(C_HN + 16 * l, groups)
            nh = cfg.get("nh", 16)
            W = {}

            def load_qf(h):
                W[h] = {}
                W[h]["q"] = load_w(hw_d["q"][l, h])
                W[h]["f"] = load_w(hw_d["f"][l, h])

            def load_ig(h):
                W[h]["i"] = load_w(hw_d["i"][l, h])
                W[h]["g"] = load_w(hw_d["g"][l, h])
            load_qf(0)
            glist = [(h, c0, n) for h in range(nh) for (c0, n) in groups]
            pre = None
            for idx, (h, c0, n) in enumerate(glist):
                if c0 == groups[0][0]:
                    load_ig(h)
                nxt = glist[idx + 1] if idx + 1 < len(glist) else None
                if nxt is not None and nxt[0] not in W:
                    load_qf(nxt[0])
                lbc = colsT[:, C_LB + 16 * l + h:C_LB + 16 * l + h + 1]
                omc = colsT[:, C_OMLB + 16 * l + h:C_OMLB + 16 * l + h + 1]
                nomc = colsT[:, C_NOMLB + 16 * l + h:C_NOMLB + 16 * l + h + 1]
                goc = colsT[:, C_GO + l:C_GO + l + 1]
                nxt_info = None
                if nxt is not None and cfg.get("prefetch_proj", True):
                    nh_, nc0, nn = nxt
                    nxt_info = (W[nh_]["q"], W[nh_]["f"], nc0, nn)
                (sq, kq), (sf, kf), (si, ki), (sg_, kg) = W[h]["q"], W[h]["f"], W[h]["i"], W[h]["g"]
                _hg_group(l, h, c0, n, sq, kq, sf, kf, si, ki, sg_, kg, lbc, omc, nomc, goc, pre == (h, c0, n), nxt_info)
                pre = nxt if nxt_info is not None else None
            if last_tile:
                S.op("sp", lambda e: e.dma_start(out=stp_d[l].rearrange("h k v -> k h v"), in_=Sst[:, l, :, :]),
                     reads=["S_%d_%d" % (l, h) for h in range(16)])
            out_proj(hw_d["o"][l], groups)

        def _hg_group(l, h, c0, n, sq, kq, sf, kf, si, ki, sg_, kg, lbc, omc, nomc, goc, pre_done=False, nxt_info=None):
            if True:
                if True:
                    smp = (c0 >= TP)
                    C = TS if smp else 32
                    nch = n // C
                    mk = scm[:, 512:512 + n] if smp else scm[:, 0:n]
                    q32, sgb, lfb, bb, epb, k32, = (t32[:, i, 0:n] for i in range(6))
                    qt, khb, kdb, vTb = (t16[:, i, 0:n] for i in range(4))
                    if not pre_done:
                        proj(pP[:, 0, 0:n], sq, kq, hT, ["hT"], c0, n, "pP0")
                    S.op("act", lambda e: e.activation(out=q32, in_=pP[:, 0, 0:n], func=AF.Silu), reads=["pP0"], writes=["t32_0"])
                    if not pre_done:
                        proj(pP[:, 1, 0:n], sf, kf, hT, ["hT"], c0, n, "pP1")
                    S.op("act", lambda e: e.activation(out=sgb, in_=pP[:, 1, 0:n], func=AF.Sigmoid), reads=["pP1"], writes=["t32_1"])
                    S.op("act", lambda e: e.activation(out=lfb, in_=sgb, func=AF.Ln, scale=omc, bias=lbc),
                         reads=["t32_1", "colsT"], writes=["t32_2"])
                    S.op("dve", lambda e: e.tensor_scalar_max(out=lfb, in0=lfb, scalar1=LN_MINF), reads=["t32_2"], writes=["t32_2"])
                    S.op("dve", lambda e: e.tensor_scalar(out=k32, in0=sgb, scalar1=nomc, scalar2=omc, op0=ALU.mult, op1=ALU.add),
                         reads=["t32_1", "colsT"], writes=["t32_5"])
                    S.op("dve", lambda e: e.tensor_tensor_scan(out=bb, data0=mk, data1=lfb, initial=0.0, op0=ALU.mult, op1=ALU.add),
                         reads=["t32_2", "scm"], writes=["t32_3"])
                    S.op("act", lambda e: e.activation(out=epb, in_=bb, func=AF.Exp), reads=["t32_3"], writes=["t32_4"])
                    S.op("act", lambda e: e.activation(out=sgb, in_=bb, func=AF.Exp, scale=-1.0), reads=["t32_3"], writes=["t32_1"])
                    S.op("dve", lambda e: e.scalar_tensor_tensor(out=qt, in0=q32, scalar=QSC, in1=epb, op0=ALU.mult, op1=ALU.mult),
                         reads=["t32_0", "t32_4"], writes=["t16_0"])
                    S.op("dve", lambda e: e.tensor_tensor(out=lfb, in0=k32, in1=sgb, op=ALU.mult), reads=["t32_5", "t32_1"], writes=["t32_2"])
                    S.op("act", lambda e: e.activation(out=khb, in_=lfb, func=AF.Copy), reads=["t32_2"], writes=["t16_1"])
                    ep3 = epb.rearrange("p (c t) -> p c t", t=C)
                    S.op("dve", lambda e: e.tensor_tensor(out=kdb.rearrange("p (c t) -> p c t", t=C),
                                                          in0=lfb.rearrange("p (c t) -> p c t", t=C),
                                                          in1=ep3[:, :, C - 1:C].to_broadcast([128, nch, C]), op=ALU.mult),
                         reads=["t32_2", "t32_4"], writes=["t16_2"])
                    proj(pP[:, 0, 0:n], si, ki, hT, ["hT"], c0, n, "pP0")
                    S.op("act", lambda e: e.activation(out=vTb, in_=pP[:, 0, 0:n], func=AF.Copy), reads=["pP0"], writes=["t16_3"])
                    g32 = t32[:, 3, 0:n]
                    proj(pP[:, 1, 0:n], sg_, kg, hT, ["hT"], c0, n, "pP1")
                    S.op("act", lambda e: e.activation(out=g32, in_=pP[:, 1, 0:n], func=AF.Silu), reads=["pP1"], writes=["t32_3"])
                    extra = []
                    extra_rd, extra_wr = [], []
                    if nxt_info is not None:
                        (nsq, nkq), (nsf, nkf), nc0, nn = nxt_info
                        extra = proj_fns(pP[:, 0, 0:nn], nsq, hT, nc0, nn) + proj_fns(pP[:, 1, 0:nn], nsf, hT, nc0, nn)
                        extra_rd = list(nkq) + list(nkf) + ["hT"]
                        extra_wr = ["pP0", "pP1"]
                    per_it = -(-len(extra) // nch) if extra else 0

                    def _pe_prep_fns(ci):
                        a0 = ci * C
                        pb = ci % 2
                        ptk = pTb[0:C, pb * 256:pb * 256 + 128]
                        ptv = pTb[0:C, pb * 256 + 128:pb * 256 + 256]
                        pA = pM[0:C, 256 + pb * 32:256 + pb * 32 + C]
                        return [lambda e, a0=a0, ptk=ptk: e.transpose(ptk, kdb[:, a0:a0 + C], ident_b),
                                lambda e, a0=a0, ptv=ptv: e.transpose(ptv, vTb[:, a0:a0 + C], ident_b),
                                lambda e, a0=a0, pA=pA: e.matmul(pA, lhsT=khb[:, a0:a0 + C], rhs=qt[:, a0:a0 + C], start=True, stop=True)]

                    def _post_prep(ci):
                        pb = ci % 2
                        pA = pM[0:C, 256 + pb * 32:256 + pb * 32 + C]
                        S.op("act", (lambda e, pb=pb: e.activation(out=kvtm[0:C, pb, :, :].rearrange("p a b -> p (a b)"),
                                                                    in_=pTb[0:C, pb * 256:pb * 256 + 256], func=AF.Copy)),
                             reads=["pTb%d" % pb], writes=["kvtm%d" % pb])
                        S.op("dve", (lambda e, pA=pA, pb=pb: e.tensor_tensor(out=A_sb[0:C, pb, 0:C], in0=pA, in1=msk[0:C, 3, 0:C], op=ALU.mult)),
                             reads=["pA%d" % pb, "msk"], writes=["A_sb%d" % pb])

                    def _iter(ci):
                        a0 = ci * C
                        pb = ci % 2
                        nb = (ci + 1) % 2
                        if smp:
                            s_idx = ci
                            s_in, s_out = Ssmp[:, pb, :], Ssmo[:, pb, :]
                            k_in, k_out = "Ssmp%d" % pb, "Ssmo%d" % pb
                            S.op("sp", (lambda e, s_idx=s_idx, s_in=s_in: e.dma_start(out=s_in, in_=st_in[l, s_idx, h])), writes=[k_in])
                        else:
                            bufs = [(Sst[:, l, h, :], "S_%d_%d" % (l, h)), (Stmp[:, :], "Stmp")]
                            (s_in, k_in), (s_out, k_out) = bufs[ci % 2], bufs[(ci + 1) % 2]
                        sbf = S_bf[:, pb, :]
                        S.op("act", (lambda e, sbf=sbf, s_in=s_in: e.activation(out=sbf, in_=s_in, func=AF.Copy)),
                             reads=[k_in], writes=["S_bf%d" % pb])
                        pS = pM[:, pb * 128:(pb + 1) * 128]
                        fns = []
                        rd = ["kvtm%d" % pb]
                        wr_ = ["pS%d" % pb]
                        if ci + 1 < nch:
                            fns += _pe_prep_fns(ci + 1)
                            rd += ["t16_0", "t16_1", "t16_2", "t16_3", "cstb"]
                            wr_ += ["pTb%d" % nb, "pA%d" % nb]
                        fns.append(lambda e, pb=pb, pS=pS: e.matmul(pS, lhsT=kvtm[0:C, pb, 0, :], rhs=kvtm[0:C, pb, 1, :], start=True, stop=True))
                        S.pe(fns, reads=rd, writes=wr_)
                        if ci + 1 < nch:
                            _post_prep(ci + 1)
                        ex = extra[ci * per_it:(ci + 1) * per_it]
                        S.pe([lambda e, a0=a0, pb=pb: e.matmul(pO[:, a0:a0 + C], lhsT=kvtm[0:C, pb, 1, :], rhs=A_sb[0:C, pb, 0:C],
                                                               start=True, stop=False),
                              lambda e, a0=a0, sbf=sbf: e.matmul(pO[:, a0:a0 + C], lhsT=sbf, rhs=qt[:, a0:a0 + C], start=False, stop=True)] + ex,
                             reads=["kvtm%d" % pb, "A_sb%d" % pb, "S_bf%d" % pb, "t16_0"] + (extra_rd if ex else []),
                             writes=["pO"] + (extra_wr if ex else []))
                        S.op("dve", (lambda e, s_in=s_in, s_out=s_out, pS=pS, a0=a0: e.scalar_tensor_tensor(
                            out=s_out, in0=s_in, scalar=epb[:, a0 + C - 1:a0 + C], in1=pS, op0=ALU.mult, op1=ALU.add)),
                            reads=["pS%d" % pb, "t32_4", k_in], writes=[k_out])
                        if smp:
                            S.op("sp", (lambda e, s_idx=s_idx, s_out=s_out: e.dma_start(out=sts_d[l, s_idx, h], in_=s_out)),
                                 reads=[k_out])
                    S.pe(_pe_prep_fns(0), reads=["t16_0", "t16_1", "t16_2", "t16_3", "cstb"], writes=["pTb0", "pA0"])
                    _post_prep(0)
                    for ci in range(nch):
                        _iter(ci)
                    osq = sqb[:, 0, 0:n]
                    S.op("act", lambda e: e.activation(out=osq, in_=pO[:, 0:n], func=AF.Square), reads=["pO"], writes=["sqb0"])
                    S.pe([lambda e: e.matmul(pN[:, 0:n], lhsT=ones_b, rhs=osq, start=True, stop=True)], reads=["sqb0", "cstb"], writes=["pN"])
                    rstd_from(pN[:, 0:n], n, 1.0 / 128, "pN")
                    S.op("dve", lambda e: e.scalar_tensor_tensor(out=q32, in0=pO[:, 0:n], scalar=goc, in1=rstd[:, 0:n],
                                                                 op0=ALU.mult, op1=ALU.mult),
                         reads=["pO", "rstd", "colsT"], writes=["t32_0"])
                    S.op("dve", lambda e: e.tensor_tensor(out=ogT[:, h * TW + c0:h * TW + c0 + n], in0=q32, in1=g32, op=ALU.mult),
                         reads=["t32_0", "t32_3"], writes=["ogT"])

        def out_proj(w_dram, groups):
            og3 = ogT[:].rearrange("p (a b) -> p a b", b=TW)
            for fo in range(16):
                so, ko = load_w(w_dram[fo])
                for gi, (c0, n) in enumerate(groups):
                    pb = (fo * 3 + gi) % 2
                    proj(pP[:, pb, 0:n], so, ko, og3, ["ogT"], c0, n, "pP%d" % pb)
                    S.op("dve", (lambda e, pb=pb, fo=fo, c0=c0, n=n: e.tensor_tensor(out=xT[:, fo, c0:c0 + n], in0=xT[:, fo, c0:c0 + n],
                                                                                      in1=pP[:, pb, 0:n], op=ALU.add)),
                         reads=["pP%d" % pb, "xT"], writes=["xT"])

        def ffn_layer(l, groups):
            rmsnorm_to_h(C_FN + 16 * l, groups)
            ng = len(groups)
            def hid(fb, gi, j, n):
                o = ((fb * 3 + gi) * 4 + j) * 512
                return ogT[:, o:o + n]
            for fc in range(cfg.get("nfc", NFC)):
                fb = fc % 2
                for j in range(4):
                    f = fc * 4 + j
                    sg_, kg = load_w(wgate_d[l, f])
                    su, ku = load_w(wup_d[l, f])
                    for gi, (c0, n) in enumerate(groups):
                        pb = (j * 3 + gi) % 2
                        proj(pP[:, pb, 0:n], sg_, kg, hT, ["hT"], c0, n, "pP%d" % pb)
                        proj(pU[:, pb, 0:n], su, ku, hT, ["hT"], c0, n, "pU%d" % pb)
                        gs = t32[:, pb, 0:n]
                        S.op("act", (lambda e, gs=gs, pb=pb, n=n: e.activation(out=gs, in_=pP[:, pb, 0:n], func=AF.Silu)),
                             reads=["pP%d" % pb], writes=["t32_%d" % pb])
                        S.op("dve", (lambda e, gs=gs, pb=pb, n=n, fb=fb, gi=gi, j=j: e.tensor_tensor(
                            out=hid(fb, gi, j, n), in0=gs, in1=pU[:, pb, 0:n], op=ALU.mult)),
                            reads=["t32_%d" % pb, "pU%d" % pb], writes=["hid%d_%d" % (fb, gi)])
                sd, kd = load_w(wdown_d[l, fc], nslots=4)
                for fo in range(16):
                    for gi, (c0, n) in enumerate(groups):
                        pb = (fo * 3 + gi) % 2
                        fns = []
                        for j in range(4):
                            fns.append(lambda e, j=j, pb=pb, fo=fo, gi=gi, n=n, sd=sd, fb=fb: e.matmul(
                                pP[:, pb, 0:n], lhsT=wr[:, sd + j, fo * 128:(fo + 1) * 128], rhs=hid(fb, gi, j, n),
                                start=(j == 0), stop=(j == 3)))
                        S.pe(fns, reads=kd + ["hid%d_%d" % (fb, gi)], writes=["pP%d" % pb])
                        S.op("dve", (lambda e, pb=pb, fo=fo, c0=c0, n=n: e.tensor_tensor(out=xT[:, fo, c0:c0 + n], in0=xT[:, fo, c0:c0 + n],
                                                                                          in1=pP[:, pb, 0:n], op=ALU.add)),
                             reads=["pP%d" % pb, "xT"], writes=["xT"])

        def aview(off_words, nwords, dt):
            v = arena[:, off_words:off_words + nwords]
            return v.bitcast(dt) if dt != F32 else v
        KW = 128 + TW
        kT = aview(0, 2368, BF16).rearrange("p (a b) -> p a b", b=KW)
        vaug = aview(2368, 2340, BF16).rearrange("p (a b c) -> p a b c", b=8, c=65)
        vsn = aview(4708, 1040, BF16).rearrange("p (a b c) -> p a b c", b=8, c=65)
        kck = aview(5748, 256, BF16).rearrange("p (a b) -> p a b", b=128)
        vck = aview(6004, 260, BF16).rearrange("p (a b c) -> p a b c", b=2, c=65)
        otm = aview(6264, 64, BF16)
        pTs = aview(6328, 256, BF16).rearrange("p (a b c) -> p a b c", b=2, c=128)
        scs = aview(6584, 512, F32).rearrange("p (a b c) -> p a b c", b=2, c=128)
        qTc = aview(7096, 528, BF16)
        esk = aview(7624, 64, F32)
        skb = aview(7688, 64, F32)
        rcs = aview(7752, 4, F32)

        def rope_apply(src, srckey, n, c0, halo):
            if halo:
                S.op("sp", lambda e: e.dma_start(out=t32[:, 4, 0:n], in_=rope_d[2][:, 0:n]), writes=["t32_4"])
                S.op("sp", lambda e: e.dma_start(out=t32[:, 5, 0:n], in_=rope_d[3][:, 0:n]), writes=["t32_5"])
            else:
                S.op("sp", lambda e: e.dma_start(out=t32[:, 4, 0:n], in_=rope_d[0][:, c0:c0 + n]), writes=["t32_4"])
                S.op("sp", lambda e: e.dma_start(out=t32[:, 5, 0:n], in_=rope_d[1][:, c0:c0 + n]), writes=["t32_5"])
            S.pe([lambda e: e.matmul(pN[:, 0:n], lhsT=rotP, rhs=src, start=True, stop=True)], reads=[srckey, "cst"], writes=["pN"])
            S.op("dve", lambda e: e.tensor_tensor(out=t32[:, 2, 0:n], in0=pN[:, 0:n], in1=t32[:, 5, 0:n], op=ALU.mult),
                 reads=["pN", "t32_5"], writes=["t32_2"])
            S.op("dve", lambda e: e.tensor_tensor(out=t32[:, 3, 0:n], in0=src, in1=t32[:, 4, 0:n], op=ALU.mult),
                 reads=[srckey, "t32_4"], writes=["t32_3"])
            S.op("dve", lambda e: e.tensor_tensor(out=t32[:, 3, 0:n], in0=t32[:, 3, 0:n], in1=t32[:, 2, 0:n], op=ALU.add),
                 reads=["t32_3", "t32_2"], writes=["t32_3"])

        def normed_rot(pin, pkey, n, c0, halo, gcol):
            sq32 = t32[:, 0, 0:n]
            kn = t32[:, 1, 0:n]
            S.op("act", lambda e: e.activation(out=sq32, in_=pin, func=AF.Square), reads=[pkey], writes=["t32_0"])
            S.pe([lambda e: e.matmul(pN[:, 0:n], lhsT=bd64, rhs=sq32, start=True, stop=True)], reads=["t32_0", "cst"], writes=["pN"])
            rstd_from(pN[:, 0:n], n, 1.0 / 64, "pN")
            S.op("dve", lambda e: e.scalar_tensor_tensor(out=kn, in0=pin, scalar=colsT[:, gcol:gcol + 1], in1=rstd[:, 0:n],
                                                         op0=ALU.mult, op1=ALU.mult),
                 reads=[pkey, "rstd", "colsT"], writes=["t32_1"])
            rope_apply(kn, "t32_1", n, c0, halo)

        def kv_compute(groups, halo):
            kvs = cfg.get("kvs", 99)
            rmsnorm_to_h(C_KVN, groups)
            for m in range(4):
                sk, kk = load_w(wk_d[m])
                for (c0, n) in groups:
                    proj(pP[:, 0, 0:n], sk, kk, hT, ["hT"], c0, n, "pP0")
                    if kvs < 2:
                        continue
                    normed_rot(pP[:, 0, 0:n], "pP0", n, c0, halo, C_KN)
                    if kvs < 3:
                        continue
                    if halo:
                        S.op("act", (lambda e, m=m, n=n: e.activation(out=khalo[:, m, :], in_=t32[:, 3, 0:n], func=AF.Copy)),
                             reads=["t32_3"], writes=["khalo"])
                    else:
                        S.op("act", (lambda e, m=m, c0=c0, n=n: e.activation(out=kT[:, m, 128 + c0:128 + c0 + n], in_=t32[:, 3, 0:n], func=AF.Copy)),
                             reads=["t32_3"], writes=["kT"])
                        if kvs < 4:
                            continue
                        if c0 == 512:
                            S.op("sp", (lambda e, m=m: e.dma_start(out=kwp_d[m], in_=t32[:, 3, 384:512])), reads=["t32_3"])
                        if c0 == TP:
                            S.op("sp", (lambda e, m=m, n=n: e.dma_start(out=kws_new[m], in_=t32[:, 3, 0:n])), reads=["t32_3"])
            for fo in range(4 if kvs >= 5 else 0):
                sv, kv = load_w(wv_d[fo])
                for (c0, n) in groups:
                    proj(pP[:, 1, 0:n], sv, kv, hT, ["hT"], c0, n, "pP1")
                    if halo:
                        S.op("act", (lambda e, fo=fo, n=n: e.activation(out=vhalo[:, fo, :], in_=pP[:, 1, 0:n], func=AF.Copy)),
                             reads=["pP1"], writes=["vhalo"])
                        continue
                    v32 = t32[:, 4, 0:n]
                    vb = t16[:, 0, 0:n]
                    S.op("act", (lambda e, v32=v32, n=n: e.activation(out=v32, in_=pP[:, 1, 0:n], func=AF.Copy)), reads=["pP1"], writes=["t32_4"])
                    S.op("act", (lambda e, vb=vb, v32=v32: e.activation(out=vb, in_=v32, func=AF.Copy)), reads=["t32_4"], writes=["t16_0"])
                    if kvs < 6:
                        continue
                    if c0 == 512:
                        S.op("sp", (lambda e, fo=fo: e.dma_start(out=vwp_d[fo], in_=t32[:, 4, 384:512])), reads=["t32_4"])
                    if c0 == TP:
                        S.op("sp", (lambda e, fo=fo, n=n: e.dma_start(out=vws_new[fo], in_=t32[:, 4, 0:n])), reads=["t32_4"])
                    if kvs < 7:
                        continue
                    if c0 == TP:
                        for s in range(NS):
                            S.pe([lambda e, s=s, vb=vb: e.transpose(pTb[0:TS, 0:128], vb[:, s * TS:(s + 1) * TS], ident_b)],
                                 reads=["t16_0", "cstb"], writes=["pTb0"])
                            S.op("act", (lambda e, s=s, fo=fo: e.activation(out=vsn[0:TS, s, 2 * fo:2 * fo + 2, 0:64],
                                                                             in_=pTb[0:TS, 0:128].rearrange("p (a b) -> p a b", b=64), func=AF.Copy)),
                                 reads=["pTb0"], writes=["vsn"])
                    else:
                        for bi in range(n // 128):
                            kb = 1 + (c0 + bi * 128) // 128
                            S.pe([lambda e, bi=bi, vb=vb: e.transpose(pTb[:, 0:128], vb[:, bi * 128:(bi + 1) * 128], ident_b)],
                                 reads=["t16_0", "cstb"], writes=["pTb0"])
                            S.op("act", (lambda e, kb=kb, fo=fo: e.activation(out=vaug[:, kb, 2 * fo:2 * fo + 2, 0:64],
                                                                               in_=pTb[:, 0:128].rearrange("p (a b) -> p a b", b=64), func=AF.Copy)),
                                 reads=["pTb0"], writes=["vaug"])

        def load_cache(m):
            for s in range(NS):
                ctm = t32[:, 0:2, 0:128]
                S.op("sp", (lambda e, s=s: e.dma_start(out=t32[:, 0, 0:128], in_=ck_d[s][:, m * 128:(m + 1) * 128])), writes=["t32_0"])
                S.op("sp", (lambda e, s=s: e.dma_start(out=t32[:, 1, 0:128], in_=cv_d[s][:, m * 128:(m + 1) * 128])), writes=["t32_1"])
                S.op("act", (lambda e, s=s: e.activation(out=vck[:, s, :, 0:64], in_=t32[:, 1, 0:128].rearrange("p (g d) -> p g d", d=64), func=AF.Copy)),
                     reads=["t32_1"], writes=["vck"])
                S.op("act", (lambda e: e.activation(out=t16[:, 1, 0:128], in_=t32[:, 0, 0:128], func=AF.Copy)), reads=["t32_0"], writes=["t16_1"])
                S.pe([lambda e: e.transpose(pTb[:, 256:384], t16[:, 1, 0:128], ident_b)], reads=["t16_1", "cstb"], writes=["pTb1"])
                S.op("act", (lambda e, s=s: e.activation(out=kck[:, s, :], in_=pTb[:, 256:384], func=AF.Copy)), reads=["pTb1"], writes=["kck"])

        def attn_layer(j, groups):
            rmsnorm_to_h(C_AN + 16 * j, groups)
            og3 = ogT[:].rearrange("p (a b) -> p a b", b=TW)
            it = [0]

            def attend(nq, qc0, fo, m, prevK, prevV, curK, curV, nk, mprev, mcur, pkeys):
                for hh in range(2):
                    half = hh * 64
                    g = 2 * m + hh
                    hq = 8 * m + 4 * hh + (fo % 4)
                    pb = it[0] % 2
                    it[0] += 1
                    qsl = qTc[half:half + 64, qc0:qc0 + nq]
                    sc = scs[:, pb]
                    pT_ = pTs[:, pb]
                    pS0 = pU[:, pb, 0:nq]
                    pS1 = pU[0:nk, pb, 128:128 + nq]
                    S.pe([lambda e, half=half, qsl=qsl, pS0=pS0: e.matmul(pS0, lhsT=prevK(half), rhs=qsl, start=True, stop=True),
                          lambda e, half=half, qsl=qsl, pS1=pS1: e.matmul(pS1, lhsT=curK(half), rhs=qsl, start=True, stop=True)],
                         reads=["qTc", "kT"] + pkeys, writes=["pU%d" % pb])
                    S.op("dve", (lambda e, sc=sc, pS0=pS0: e.scalar_tensor_tensor(out=sc[:, 0, 0:nq], in0=pS0, scalar=ASC, in1=mprev[:, 0:nq],
                                                                                   op0=ALU.mult, op1=ALU.add)),
                         reads=["pU%d" % pb, "msk"], writes=["sc%d" % pb])
                    S.op("dve", (lambda e, sc=sc, pS1=pS1: e.scalar_tensor_tensor(out=sc[0:nk, 1, 0:nq], in0=pS1, scalar=ASC, in1=mcur[0:nk, 0:nq],
                                                                                   op0=ALU.mult, op1=ALU.add)),
                         reads=["pU%d" % pb, "msk"], writes=["sc%d" % pb])
                    S.op("act", (lambda e, sc=sc, pT_=pT_: e.activation(out=pT_[:, 0, 0:nq], in_=sc[:, 0, 0:nq], func=AF.Exp)),
                         reads=["sc%d" % pb], writes=["pT%d" % pb])
                    S.op("act", (lambda e, sc=sc, pT_=pT_: e.activation(out=pT_[0:nk, 1, 0:nq], in_=sc[0:nk, 1, 0:nq], func=AF.Exp)),
                         reads=["sc%d" % pb], writes=["pT%d" % pb])
                    po = pO[0:nq, pb * 128:pb * 128 + 65]
                    S.pe([lambda e, g=g, pT_=pT_, po=po: e.matmul(po, lhsT=pT_[:, 0, 0:nq], rhs=prevV(g), start=True, stop=False),
                          lambda e, g=g, pT_=pT_, po=po: e.matmul(po, lhsT=pT_[0:nk, 1, 0:nq], rhs=curV(g), start=False, stop=True)],
                         reads=["pT%d" % pb, "vaug", "vsn"] + pkeys, writes=["pO%d" % pb])
                    rc = rcs[0:nq, pb:pb + 1]
                    S.op("dve", (lambda e, rc=rc, po=po, hq=hq: e.tensor_tensor(out=rc, in0=po[:, 64:65], in1=esk[0:nq, j * 32 + hq:j * 32 + hq + 1],
                                                                                 op=ALU.add)),
                         reads=["pO%d" % pb, "esk"], writes=["rc%d" % pb])
                    S.op("dve", (lambda e, rc=rc: e.reciprocal(out=rc, in_=rc)), reads=["rc%d" % pb], writes=["rc%d" % pb])
                    S.op("dve", (lambda e, rc=rc, po=po, half=half: e.tensor_scalar(out=otm[0:nq, half:half + 64], in0=po[:, 0:64], scalar1=rc,
                                                                                     scalar2=None, op0=ALU.mult)),
                         reads=["pO%d" % pb, "rc%d" % pb], writes=["otm"])
                S.pe([lambda e: e.transpose(pTb[:, 512:512 + nq], otm[0:nq, :], ident_b[0:nq, 0:nq])], reads=["otm", "cstb"], writes=["pTb2"])
                S.op("act", (lambda e: e.activation(out=og3[:, fo, qc0:qc0 + nq], in_=pTb[:, 512:512 + nq], func=AF.Copy)),
                     reads=["pTb2"], writes=["ogT"])

            for m in range(cfg.get("mstart", 0), cfg.get("nm", 4)):
                if cfg.get("astop", 99) >= 5:
                    load_cache(m)
                for i in range(4 if cfg.get("astop", 99) >= 6 else 0):
                    fo = 4 * m + i
                    sq_, kq = load_w(awq_d[j, fo])
                    for (c0, n) in groups:
                        proj(pP[:, 0, 0:n], sq_, kq, hT, ["hT"], c0, n, "pP0")
                        normed_rot(pP[:, 0, 0:n], "pP0", n, c0, False, C_QN + j)
                        S.op("act", (lambda e, c0=c0, n=n: e.activation(out=qTc[:, c0:c0 + n], in_=t32[:, 3, 0:n], func=AF.Copy)),
                             reads=["t32_3"], writes=["qTc"])
                    for qb in range(8 if cfg.get("astop", 99) >= 7 else 0):
                        k0 = qb * 128
                        attend(128, qb * 128, fo, m,
                               lambda half, k0=k0, m=m: kT[half:half + 64, m, k0:k0 + 128],
                               lambda g, qb=qb: vaug[:, qb, g, :],
                               lambda half, k0=k0, m=m: kT[half:half + 64, m, k0 + 128:k0 + 256],
                               lambda g, qb=qb: vaug[:, qb + 1, g, :],
                               128, msk[:, 2, :] if qb == 0 else msk[:, 0, :], msk[:, 1, :], [])
                    for s in range(NS if cfg.get("astop", 99) >= 8 else 0):
                        c1 = 128 + TP + s * TS
                        attend(TS, TP + s * TS, fo, m,
                               lambda half, s=s: kck[half:half + 64, s, :],
                               lambda g, s=s: vck[:, s, g % 2, :],
                               lambda half, c1=c1, m=m: kT[half:half + 64, m, c1:c1 + TS],
                               lambda g, s=s: vsn[0:TS, s, g, :],
                               TS, msk[:, 0, :], msk[:, 1, :], ["kck", "vck"])
            if cfg.get("astop", 99) >= 9:
                out_proj(awo_d[j], groups)

        xs3 = xs_d.rearrange("p (a b) -> p a b", b=NS * TS)
        for t in range(NT - n_tiles, NT):
            last = (t == NT - 1)
            groups = list(groups_main) + ([(TP, NS * TS)] if last else [])
            S.op("sp", (lambda e, t=t: e.dma_start(out=xT[:, :, 0:TP], in_=xin[t].rearrange("p (a b) -> p a b", b=TP))),
                 writes=["xT"], reads=[])
            if last:
                S.op("sp", lambda e: e.dma_start(out=xT[:, :, TP:TW], in_=xs3), writes=["xT"])
            if t == NT - n_tiles:
                S.op("dve", lambda e: e.memset(arena[:, 0:4096], 0.0),
                     writes=["S_%d_%d" % (l, h) for l in range(2) for h in range(16)])
                S.op("dve", lambda e: e.memset(arena[:, 7760:8272], 0.0), writes=["khalo", "vhalo"])
            for l in range(cfg.get("nl", 2)):
                if cfg.get("hg", True):
                    hgrn_layer(l, groups, last)
                if cfg.get("ffn", True):
                    ffn_layer(l, groups)
            if do_attn and t == NT - 2:
                kv_compute([(TP - 128, 128)], True)
        if do_attn:
            last_groups = list(groups_main) + [(TP, NS * TS)]
            S.barrier()
            S.op("sp", lambda e: e.dma_start(out=skb, in_=cols_d[:, C_SK:C_SK + 64]), writes=["skb"])
            S.op("act", lambda e: e.activation(out=esk, in_=skb, func=AF.Exp), reads=["skb"], writes=["esk"])
            S.op("dve", lambda e: e.memset(vaug[:, :, :, 64:65], 1.0), writes=["vaug"])
            S.op("dve", lambda e: e.memset(vsn[:, :, :, 64:65], 1.0), writes=["vsn"])
            S.op("dve", lambda e: e.memset(vck[:, :, :, 64:65], 1.0), writes=["vck"])
            S.op("act", lambda e: e.activation(out=kT[:, :, 0:128], in_=khalo, func=AF.Copy), reads=["khalo"], writes=["kT"])
            for fo in range(4):
                S.pe([lambda e, fo=fo: e.transpose(pTb[:, 0:128], vhalo[:, fo, :], ident_b)], reads=["vhalo", "cstb"], writes=["pTb0"])
                S.op("act", (lambda e, fo=fo: e.activation(out=vaug[:, 0, 2 * fo:2 * fo + 2, 0:64],
                                                            in_=pTb[:, 0:128].rearrange("p (a b) -> p a b", b=64), func=AF.Copy)),
                     reads=["pTb0"], writes=["vaug"])
            if cfg.get("astop", 99) >= 2:
                kv_compute(last_groups, False)
            if cfg.get("astop", 99) >= 3:
                for s in range(NS):
                    S.op("sp", (lambda e, s=s: e.dma_start(out=kws_old[s], in_=ck_d[s, TS:128, :])))
                    S.op("sp", (lambda e, s=s: e.dma_start(out=vws_old[s], in_=cv_d[s, TS:128, :])))
            for j in range(cfg.get("nl", 2) if cfg.get("astop", 99) >= 4 else 0):
                attn_layer(j, last_groups)
                if cfg.get("ffn", True):
                    ffn_layer(2 + j, last_groups)
        S.op("sp", lambda e: e.dma_start(out=yT_d.rearrange("p (a b) -> p a b", b=TW), in_=xT[:]), reads=["xT"])
        S.finish()
        with nc.Block() as block:
            S.emit(block)
    return nc


_NC_CACHE = {}


def _relayout_w(w, ncol_chunks):
    K, F = w.shape
    a = w.reshape(K // 128, 128, F // 128, 128)
    return np.ascontiguousarray(a.transpose(2, 1, 0, 3)).reshape(F // 128, 128, (K // 128) * 128)


def kernel(x_prompt, x_sample, state_hgrn, cache_k_win, cache_v_win,
           hgrn_norm, hgrn_wq, hgrn_wf, hgrn_wi, hgrn_wg, hgrn_lb_logits, hgrn_onorm, hgrn_wo,
           kv_norm, w_k, w_v, k_norm,
           attn_norm, attn_wq, q_norm, sinks, attn_wo,
           ffn_norm, w_gate, w_up, w_down, _cfg=None):
    f = lambda a: np.asarray(a, dtype=np.float32)
    x_prompt, x_sample, state_hgrn = f(x_prompt), f(x_sample), f(state_hgrn)
    cache_k_win, cache_v_win = f(cache_k_win), f(cache_v_win)
    ncores = (_cfg or {}).get("ncores", 8)
    shared = {}
    for nm, w in (("q", hgrn_wq), ("f", hgrn_wf), ("i", hgrn_wi), ("g", hgrn_wg), ("o", hgrn_wo)):
        w = f(w)
        shared["hw_" + nm] = np.stack([_relayout_w(w[l], 16) for l in range(2)])
    shared["wgate"] = np.stack([_relayout_w(f(w_gate)[l], NF) for l in range(4)])
    shared["wup"] = np.stack([_relayout_w(f(w_up)[l], NF) for l in range(4)])
    wd = f(w_down)
    shared["wdown"] = np.ascontiguousarray(wd.reshape(4, NFC, 4, 128, 2048))
    perm = []
    for fo in range(16):
        m, i = fo // 4, fo % 4
        for hq in (8 * m + i, 8 * m + 4 + i):
            perm.extend(range(hq * 64, hq * 64 + 64))
    perm = np.array(perm)
    awq = f(attn_wq)[:, :, perm]
    awo = f(attn_wo)[:, perm, :]
    shared["awq"] = np.stack([_relayout_w(awq[j], 16) for j in range(2)])
    shared["awo"] = np.stack([_relayout_w(awo[j], 16) for j in range(2)])
    shared["wk"] = _relayout_w(f(w_k), 4)
    shared["wv"] = _relayout_w(f(w_v), 4)

    def colify(v):
        return f(v).reshape(16, 128).T

    cols = np.zeros((128, 320), np.float32)
    hn, an, fn_ = f(hgrn_norm), f(attn_norm), f(ffn_norm)
    for l in range(2):
        cols[:, 0 + 16 * l:16 + 16 * l] = colify(hn[l])
        cols[:, 32 + 16 * l:48 + 16 * l] = colify(an[l])
    for l in range(4):
        cols[:, 64 + 16 * l:80 + 16 * l] = colify(fn_[l])
    cols[:, 128:144] = colify(kv_norm)
    cols[:, 240:242] = f(hgrn_onorm).T
    cols[:, 242] = np.tile(f(k_norm), 2)
    cols[:, 243:245] = np.tile(f(q_norm), (1, 2)).T
    cols[:, 256:320] = np.broadcast_to(f(sinks).reshape(1, 64), (128, 64))
    shared["cols_base"] = cols
    shared["lb_logits"] = np.ascontiguousarray(f(hgrn_lb_logits).reshape(2, 16, 128).transpose(2, 0, 1)).reshape(128, 32)

    cst = np.zeros((4, 128, 128), np.float32)
    cst[0] = np.eye(128)
    cst[1] = 1.0
    cst[2, :64, :64] = 1.0
    cst[2, 64:, 64:] = 1.0
    for mm in range(128):
        b, o = (mm // 64) * 64, mm % 64
        if o < 32:
            cst[3, b + o + 32, mm] = -1.0
        else:
            cst[3, b + o - 32, mm] = 1.0
    kk = np.arange(128)[:, None]
    qq = np.arange(128)[None, :]
    masks = np.zeros((4, 128, 128), np.float32)
    masks[0] = np.where(kk > qq, 0.0, NEG)
    masks[1] = np.where(kk <= qq, 0.0, NEG)
    masks[3, :32, :32] = (np.arange(32)[:, None] <= np.arange(32)[None, :]).astype(np.float32)
    scm = np.ones((128, 544), np.float32)
    scm[:, 0:512:32] = 0.0
    scm[:, 512:544:8] = 0.0
    half = 32
    inv = (10000.0 ** (-np.arange(half, dtype=np.float32) / half)).astype(np.float32)
    invp = inv[np.arange(128) % 32][:, None]

    in_maps = []
    for c in range(ncores):
        seq, r = c // 4, c % 4
        nreal = (r + 1) * TP
        xp = np.zeros((SEQ, D), np.float32)
        xp[SEQ - nreal:] = x_prompt[seq, :nreal]
        xin = np.ascontiguousarray(xp.reshape(NT, TP, 16, 128).transpose(0, 3, 2, 1)).reshape(NT, 128, 16 * TP)
        xsm = x_sample[NS * c:NS * (c + 1)].reshape(NS * TS, 16, 128)
        xs = np.ascontiguousarray(xsm.transpose(2, 1, 0)).reshape(128, 16 * NS * TS)
        pos = np.concatenate([r * TP + np.arange(TP), np.tile(PAST + np.arange(TS), NS)]).astype(np.float32)
        posh = (r * TP - 128 + np.arange(128)).astype(np.float32)
        rope = np.zeros((4, 128, TW), np.float32)
        ang = invp * pos[None, :]
        rope[0], rope[1] = np.cos(ang), np.sin(ang)
        angh = invp * posh[None, :]
        rope[2, :, :128], rope[3, :, :128] = np.cos(angh), np.sin(angh)
        mk = masks.copy()
        mk[2] = masks[0] if r > 0 else NEG
        m = dict(shared)
        m.pop("cols_base"); m.pop("lb_logits")
        m.update(xin=xin, xs=xs,
                 st_in=np.ascontiguousarray(state_hgrn[:, NS * c:NS * (c + 1)]),
                 ck=np.ascontiguousarray(cache_k_win[NS * c:NS * (c + 1)].reshape(NS, 128, 512)),
                 cv=np.ascontiguousarray(cache_v_win[NS * c:NS * (c + 1)].reshape(NS, 128, 512)),
                 cols=cols, lbl=shared["lb_logits"], rope=rope, masks=mk, cst=cst, scm=scm)
        in_maps.append(m)

    key = repr(_cfg)
    if key not in _NC_CACHE:
        _NC_CACHE[key] = build_nc(_cfg)
    nc = _NC_CACHE[key]
    res = run_bass_kernel_spmd(nc, in_maps, core_ids=list(range(ncores)), trace=bool((_cfg or {}).get("trace", False)))
    if (_cfg or {}).get("trace"):
        print("EXEC_TIME_NS", res.exec_time_ns)
    R = res.results
    global _LAST_R
    _LAST_R = R
    y_prompt = np.zeros((2, SEQ, D), np.float32)
    y_sample = np.zeros((32, TS, D), np.float32)
    st_p = np.zeros((2, 2, 16, 128, 128), np.float32)
    st_s = np.zeros((2, 32, 16, 128, 128), np.float32)
    kw_p = np.zeros((2, 128, 8, 64), np.float32)
    vw_p = np.zeros((2, 128, 8, 64), np.float32)
    kw_s = np.zeros((32, 128, 8, 64), np.float32)
    vw_s = np.zeros((32, 128, 8, 64), np.float32)
    for c in range(ncores):
        seq, r = c // 4, c % 4
        yT = R[c]["yT"].reshape(128, 16, TW)
        ytm = yT.transpose(2, 1, 0).reshape(TW, D)
        y_prompt[seq, r * TP:(r + 1) * TP] = ytm[:TP]
        y_sample[NS * c:NS * (c + 1)] = ytm[TP:].reshape(NS, TS, D)
        st_s[:, NS * c:NS * (c + 1)] = R[c]["sts"]
        knew = R[c]["kws_new"].reshape(512, NS, TS).transpose(1, 2, 0)
        vnew = R[c]["vws_new"].reshape(512, NS, TS).transpose(1, 2, 0)
        kw_s[NS * c:NS * (c + 1), :120] = R[c]["kws_old"].reshape(NS, 120, 8, 64)
        vw_s[NS * c:NS * (c + 1), :120] = R[c]["vws_old"].reshape(NS, 120, 8, 64)
        kw_s[NS * c:NS * (c + 1), 120:] = knew.reshape(NS, TS, 8, 64)
        vw_s[NS * c:NS * (c + 1), 120:] = vnew.reshape(NS, TS, 8, 64)
        if r == 3:
            st_p[:, seq] = R[c]["stp"]
            kw_p[seq] = R[c]["kwp"].reshape(512, 128).T.reshape(128, 8, 64)
            vw_p[seq] = R[c]["vwp"].reshape(512, 128).T.reshape(128, 8, 64)
    return (y_prompt, y_sample, st_p, st_s, kw_p, vw_p, kw_s, vw_s)
```

```python
import numpy as np
from contextlib import ExitStack
import concourse.bass as bass
import concourse.mybir as mybir
from concourse.bass_utils import run_bass_kernel_spmd

F32, BF16 = mybir.dt.float32, mybir.dt.bfloat16
AF = mybir.ActivationFunctionType
ALU = mybir.AluOpType

D = 2048
NKC = 16
DFF = 5632
NF = 44
NFC = 11
SEQ = 4096
TP = 1024
NS = 4
TS = 8
TW = TP + NS * TS
NT = 4
PAST = 16384
EPS = 1e-6
LN_MINF = float(np.log(1e-30))
QSC = 128 ** -0.5
ASC = 64 ** -0.5
NEG = -1e30


class Sched:
    def __init__(self, nc, es):
        self.nc = nc
        self.es = es
        self.E = {}
        for n in ("pe", "act", "dve", "sp", "gq"):
            self.E[n] = dict(name=n, recs=[], seen={}, dma=n in ("sp", "gq"), pend=[])
        for n in ("pe", "act", "dve"):
            e = self.E[n]
            e["sem"] = es.enter_context(nc.semaphore(n + "_s0"))
            e["cnt"] = 0
            e["ep"] = 0
        for n in ("sp", "gq"):
            e = self.E[n]
            e["sems"] = [es.enter_context(nc.semaphore("%s_d%d" % (n, i))) for i in range(8)]
            e["tot"] = [0] * 8
            e["n"] = 0
        self.bw = {}
        self.br = {}
        self.last = {}

    def _deps(self, eng, reads, writes):
        toks = []
        for k in reads:
            if k in self.bw:
                toks.append(self.bw[k])
        for k in writes:
            if k in self.bw:
                toks.append(self.bw[k])
            toks.extend(self.br.get(k, ()))
        need = {}
        for (sem, val, src) in toks:
            if src == "pe" and eng["name"] == "pe":
                continue
            key = id(sem)
            if key not in need or need[key][1] < val:
                need[key] = (sem, val)
        waits = list(eng["pend"])
        eng["pend"] = []
        for key, (sem, val) in need.items():
            if eng["seen"].get(key, 0) < val:
                eng["seen"][key] = val
                waits.append((sem, val))
        return waits

    def op(self, en, fn, reads=(), writes=(), inc=True):
        eng = self.E[en]
        waits = self._deps(eng, reads, writes)
        if eng["dma"]:
            i = eng["n"] % len(eng["sems"])
            eng["n"] += 1
            sem = eng["sems"][i]
            if eng["tot"][i] > 0 and eng["seen"].get(id(sem), 0) < eng["tot"][i]:
                eng["seen"][id(sem)] = eng["tot"][i]
                waits.append((sem, eng["tot"][i]))
            eng["tot"][i] += 16
            tok = (sem, eng["tot"][i], en)
            eng["recs"].append((waits, fn, sem, 16))
        elif inc:
            if eng["cnt"] >= 30000:
                eng["ep"] += 1
                eng["sem"] = self.es.enter_context(self.nc.semaphore("%s_s%d" % (en, eng["ep"])))
                eng["cnt"] = 0
            eng["cnt"] += 1
            tok = (eng["sem"], eng["cnt"], en)
            eng["recs"].append((waits, fn, eng["sem"], 1))
        else:
            eng["recs"].append((waits, fn, None, 0))
            return None
        self.last[en + str(id(tok[0]))] = tok
        for k in reads:
            self.br.setdefault(k, []).append(tok)
        for k in writes:
            self.bw[k] = tok
            self.br[k] = []
        return tok

    def pe(self, fns, reads, writes):
        for f in fns[:-1]:
            self.op("pe", f, reads, writes, inc=False)
        return self.op("pe", fns[-1], reads, writes, inc=True)

    def barrier(self):
        allw = [(t[0], t[1]) for t in self.last.values()]
        for e in self.E.values():
            e["pend"] = list(allw)
            for (sem, val) in allw:
                e["seen"][id(sem)] = max(e["seen"].get(id(sem), 0), val)

    def finish(self):
        eng = self.E["sp"]
        waits = []
        for tok in self.last.values():
            waits.append((tok[0], tok[1]))
        eng["recs"].append((waits, None, None, 0))

    def emit(self, block):
        def replay(recs):
            def run(e):
                for (waits, fn, sem, amt) in recs:
                    for (s, v) in waits:
                        e.wait_ge(s, v)
                    if fn is not None:
                        ins = fn(e)
                        if sem is not None:
                            ins.then_inc(sem, amt)
            return run
        block.sync(replay(self.E["sp"]["recs"]))
        block.gpsimd(replay(self.E["gq"]["recs"]))
        block.tensor(replay(self.E["pe"]["recs"]))
        block.scalar(replay(self.E["act"]["recs"]))
        block.vector(replay(self.E["dve"]["recs"]))


def build_nc(cfg=None):
    cfg = cfg or {}
    n_tiles = cfg.get("n_tiles", NT)
    do_attn = cfg.get("attn", True)
    nc = bass.Bass("TRN2", target_bir_lowering=False, dynamic_dma_scratch_size=4096)
    es = ExitStack()
    with es:
        def din(name, shape):
            return nc.dram_tensor(name, list(shape), F32, kind="ExternalInput").ap()

        def dout(name, shape):
            return nc.dram_tensor(name, list(shape), F32, kind="ExternalOutput").ap()

        def sb(name, shape, dt):
            return es.enter_context(nc.sbuf_tensor(name, list(shape), dt))

        def ps(name, shape, dt):
            return es.enter_context(nc.psum_tensor(name, list(shape), dt))

        xin = din("xin", [NT, 128, NKC * TP])
        xs_d = din("xs", [128, NKC * NS * TS])
        st_in = din("st_in", [2, NS, 16, 128, 128])
        ck_d = din("ck", [NS, 128, 512])
        cv_d = din("cv", [NS, 128, 512])
        hw_d = {k: din("hw_" + k, [2, 16, 128, 2048]) for k in ("q", "f", "i", "g", "o")}
        wgate_d = din("wgate", [4, NF, 128, 2048])
        wup_d = din("wup", [4, NF, 128, 2048])
        wdown_d = din("wdown", [4, NFC, 4, 128, 2048])
        awq_d = din("awq", [2, 16, 128, 2048])
        awo_d = din("awo", [2, 16, 128, 2048])
        wk_d = din("wk", [4, 128, 2048])
        wv_d = din("wv", [4, 128, 2048])
        cols_d = din("cols", [128, 320])
        rope_d = din("rope", [4, 128, TW])
        mask_d = din("masks", [4, 128, 128])
        cst_d = din("cst", [4, 128, 128])
        scm_d = din("scm", [128, 544])
        lbl_d = din("lbl", [128, 32])

        yT_d = dout("yT", [128, NKC * TW])
        stp_d = dout("stp", [2, 16, 128, 128])
        sts_d = dout("sts", [2, NS, 16, 128, 128])
        kwp_d = dout("kwp", [4, 128, 128])
        vwp_d = dout("vwp", [4, 128, 128])
        kws_old = dout("kws_old", [NS, 120, 512])
        vws_old = dout("vws_old", [NS, 120, 512])
        kws_new = dout("kws_new", [4, 128, NS * TS])
        vws_new = dout("vws_new", [4, 128, NS * TS])

        xT = sb("xT", [128, NKC, TW], F32)
        hT = sb("hT", [128, NKC, TW], BF16)
        ogT = sb("ogT", [128, NKC * TW], BF16)
        arena = sb("arena", [128, 8272], F32)
        Sst = arena[:, 0:4096].rearrange("p (a b c) -> p a b c", b=16, c=128)
        khalo = arena[:, 7760:8016].bitcast(BF16).rearrange("p (a b) -> p a b", b=128)
        vhalo = arena[:, 8016:8272].bitcast(BF16).rearrange("p (a b) -> p a b", b=128)
        wr = sb("wr", [128, 6, 2048], BF16)
        t32 = sb("t32", [128, 6, 512], F32)
        t16 = sb("t16", [128, 4, 512], BF16)
        sqb = sb("sqb", [128, 1, 512], BF16)
        rstd = sb("rstd", [128, 512], F32)
        kvtm = sb("kvtm", [32, 2, 2, 128], BF16)
        A_sb = sb("A_sb", [32, 2, 32], BF16)
        S_bf = sb("S_bf", [128, 2, 128], BF16)
        Ssmp = sb("Ssmp", [128, 2, 128], F32)
        Ssmo = sb("Ssmo", [128, 2, 128], F32)
        Stmp = sb("Stmp", [128, 128], F32)
        colsT = sb("colsT", [128, 320], F32)
        ecol = sb("ecol", [128, 8], F32)
        ecol_l = sb("ecol_l", [128, 32], F32)
        cst = sb("cst_sb", [128, 4, 128], F32)
        cstb = sb("cstb", [128, 2, 128], BF16)
        msk = sb("msk", [128, 4, 128], F32)
        scm = sb("scm_sb", [128, 544], F32)

        pP = ps("pP", [128, 2, 512], F32)
        pO = ps("pO", [128, 512], F32)
        pM = ps("pM", [128, 512], F32)
        pTb = ps("pTb", [128, 1024], BF16)
        pN = ps("pN", [128, 512], F32)
        pU = ps("pU", [128, 2, 512], F32)

        S = Sched(nc, es)
        ident_b = cstb[:, 0, :]
        ones_b = cstb[:, 1, :]
        ident_f = cst[:, 0, :]
        bd64 = cst[:, 2, :]
        rotP = cst[:, 3, :]
        tri = msk[0:32, 3, 0:32]

        C_HN, C_AN, C_FN, C_KVN = 0, 32, 64, 128
        C_LB, C_OMLB, C_NOMLB = 144, 176, 208
        C_GO, C_KN, C_QN, C_SK = 240, 242, 243, 256

        S.op("sp", lambda e: e.dma_start(out=colsT[:], in_=cols_d), writes=["colsT"])
        S.op("sp", lambda e: e.dma_start(out=cst[:], in_=cst_d.rearrange("a p c -> p a c")), writes=["cst"])
        S.op("sp", lambda e: e.dma_start(out=msk[:], in_=mask_d.rearrange("a p c -> p a c")), writes=["msk"])
        S.op("sp", lambda e: e.dma_start(out=scm[:], in_=scm_d), writes=["scm"])
        S.op("act", lambda e: e.activation(out=cstb[:], in_=cst[:, 0:2, :], func=AF.Copy), reads=["cst"], writes=["cstb"])
        S.op("sp", lambda e: e.dma_start(out=ecol_l[:], in_=lbl_d), writes=["ecol_l"])
        S.op("dve", lambda e: e.tensor_tensor(out=ecol_l[:, 0:16], in0=ecol_l[:, 16:32], in1=ecol_l[:, 0:16], op=ALU.subtract),
             reads=["ecol_l"], writes=["ecol_l"])
        S.op("act", lambda e: e.activation(out=colsT[:, C_LB + 16:C_LB + 32], in_=ecol_l[:, 0:16], func=AF.Sigmoid),
             reads=["ecol_l", "colsT"], writes=["colsT"])
        S.op("dve", lambda e: e.tensor_scalar(out=colsT[:, C_OMLB:C_OMLB + 32], in0=colsT[:, C_LB:C_LB + 32], scalar1=-1.0, scalar2=1.0,
                                              op0=ALU.mult, op1=ALU.add), reads=["colsT"], writes=["colsT"])
        S.op("dve", lambda e: e.tensor_scalar(out=colsT[:, C_NOMLB:C_NOMLB + 32], in0=colsT[:, C_OMLB:C_OMLB + 32], scalar1=-1.0, scalar2=None,
                                              op0=ALU.mult), reads=["colsT"], writes=["colsT"])

        groups_main = [(0, 512), (512, 512)]
        wslot = [0]

        def load_w(dram_ap, nslots=1):
            s0 = wslot[0]
            if s0 + nslots > 6:
                s0 = 0
            wslot[0] = (s0 + nslots) % 6
            keys = ["w%d" % (s0 + i) for i in range(nslots)]
            for i in range(nslots):
                src = dram_ap[i] if nslots > 1 else dram_ap
                S.op("gq", (lambda e, i=i, src=src: e.dma_start(out=wr[:, s0 + i, :], in_=src)), writes=[keys[i]])
            return s0, keys

        def proj(pout, wslot_idx, wkeys, src, srckeys, c0, n, outkey):
            fns = []
            for kc in range(NKC):
                fns.append(lambda e, kc=kc: e.matmul(pout, lhsT=wr[:, wslot_idx, kc * 128:(kc + 1) * 128],
                                                     rhs=src[:, kc, c0:c0 + n], start=(kc == 0), stop=(kc == NKC - 1)))
            S.pe(fns, reads=list(wkeys) + list(srckeys), writes=[outkey])

        def rstd_from(psum_ap, n, inv_n, pkey):
            S.op("act", lambda e: e.activation(out=rstd[:, 0:n], in_=psum_ap, func=AF.Ln, scale=inv_n, bias=ecol[:, 0:1]),
                 reads=[pkey, "ecol"], writes=["rstd"])
            S.op("act", lambda e: e.activation(out=rstd[:, 0:n], in_=rstd[:, 0:n], func=AF.Exp, scale=-0.5),
                 reads=["rstd"], writes=["rstd"])

        S.op("dve", lambda e: e.memset(ecol[:, 0:1], EPS), writes=["ecol"])

        def rmsnorm_to_h(gcol0, groups, srcT=None):
            for (c0, n) in groups:
                _rms_group(gcol0, c0, n)

        def _rms_group(gcol0, c0, n):
            if True:
                fns = []
                for kc in range(NKC):
                    b = kc % 4
                    S.op("act", (lambda e, kc=kc, b=b: e.activation(out=t16[:, b, 0:n], in_=xT[:, kc, c0:c0 + n], func=AF.Square)),
                         reads=["xT"], writes=["t16_%d" % b])
                    S.op("pe", (lambda e, kc=kc, b=b: e.matmul(pN[:, 0:n], lhsT=ones_b, rhs=t16[:, b, 0:n],
                                                               start=(kc == 0), stop=(kc == NKC - 1))),
                         reads=["t16_%d" % b, "cstb"], writes=["pN"], inc=True)
                rstd_from(pN[:, 0:n], n, 1.0 / D, "pN")
                for kc in range(NKC):
                    S.op("dve", (lambda e, kc=kc: e.scalar_tensor_tensor(out=hT[:, kc, c0:c0 + n], in0=xT[:, kc, c0:c0 + n],
                                                                         scalar=colsT[:, gcol0 + kc:gcol0 + kc + 1], in1=rstd[:, 0:n],
                                                                         op0=ALU.mult, op1=ALU.mult)),
                         reads=["xT", "rstd", "colsT"], writes=["hT"])

        def hgrn_layer(l, groups, last_tile):
            rmsnorm_to_h(C_HN + 16 * l, groups)
            for h in range(cfg.get("nh", 16)):
                sq, kq = load_w(hw_d["q"][l, h])
                sf, kf = load_w(hw_d["f"][l, h])
                si, ki = load_w(hw_d["i"][l, h])
                sg_, kg = load_w(hw_d["g"][l, h])
                lbc = colsT[:, C_LB + 16 * l + h:C_LB + 16 * l + h + 1]
                omc = colsT[:, C_OMLB + 16 * l + h:C_OMLB + 16 * l + h + 1]
                nomc = colsT[:, C_NOMLB + 16 * l + h:C_NOMLB + 16 * l + h + 1]
                goc = colsT[:, C_GO + l:C_GO + l + 1]
                for (c0, n) in groups:
                    _hg_group(l, h, c0, n, sq, kq, sf, kf, si, ki, sg_, kg, lbc, omc, nomc, goc)
            if last_tile:
                S.op("sp", lambda e: e.dma_start(out=stp_d[l].rearrange("h k v -> k h v"), in_=Sst[:, l, :, :]),
                     reads=["S_%d_%d" % (l, h) for h in range(16)])
            out_proj(hw_d["o"][l], groups)

        def _hg_group(l, h, c0, n, sq, kq, sf, kf, si, ki, sg_, kg, lbc, omc, nomc, goc):
            if True:
                if True:
                    smp = (c0 >= TP)
                    C = TS if smp else 32
                    nch = n // C
                    mk = scm[:, 512:512 + n] if smp else scm[:, 0:n]
                    q32, sgb, lfb, bb, epb, k32, = (t32[:, i, 0:n] for i in range(6))
                    qt, khb, kdb, vTb = (t16[:, i, 0:n] for i in range(4))
                    proj(pP[:, 0, 0:n], sq, kq, hT, ["hT"], c0, n, "pP0")
                    S.op("act", lambda e: e.activation(out=q32, in_=pP[:, 0, 0:n], func=AF.Silu), reads=["pP0"], writes=["t32_0"])
                    proj(pP[:, 1, 0:n], sf, kf, hT, ["hT"], c0, n, "pP1")
                    S.op("act", lambda e: e.activation(out=sgb, in_=pP[:, 1, 0:n], func=AF.Sigmoid), reads=["pP1"], writes=["t32_1"])
                    S.op("act", lambda e: e.activation(out=lfb, in_=sgb, func=AF.Ln, scale=omc, bias=lbc),
                         reads=["t32_1", "colsT"], writes=["t32_2"])
                    S.op("dve", lambda e: e.tensor_scalar_max(out=lfb, in0=lfb, scalar1=LN_MINF), reads=["t32_2"], writes=["t32_2"])
                    S.op("dve", lambda e: e.tensor_scalar(out=k32, in0=sgb, scalar1=nomc, scalar2=omc, op0=ALU.mult, op1=ALU.add),
                         reads=["t32_1", "colsT"], writes=["t32_5"])
                    S.op("dve", lambda e: e.tensor_tensor_scan(out=bb, data0=mk, data1=lfb, initial=0.0, op0=ALU.mult, op1=ALU.add),
                         reads=["t32_2", "scm"], writes=["t32_3"])
                    S.op("act", lambda e: e.activation(out=epb, in_=bb, func=AF.Exp), reads=["t32_3"], writes=["t32_4"])
                    S.op("act", lambda e: e.activation(out=sgb, in_=bb, func=AF.Exp, scale=-1.0), reads=["t32_3"], writes=["t32_1"])
                    S.op("dve", lambda e: e.scalar_tensor_tensor(out=q32, in0=q32, scalar=QSC, in1=epb, op0=ALU.mult, op1=ALU.mult),
                         reads=["t32_0", "t32_4"], writes=["t32_0"])
                    S.op("act", lambda e: e.activation(out=qt, in_=q32, func=AF.Copy), reads=["t32_0"], writes=["t16_0"])
                    S.op("dve", lambda e: e.tensor_tensor(out=lfb, in0=k32, in1=sgb, op=ALU.mult), reads=["t32_5", "t32_1"], writes=["t32_2"])
                    S.op("act", lambda e: e.activation(out=khb, in_=lfb, func=AF.Copy), reads=["t32_2"], writes=["t16_1"])
                    ep3 = epb.rearrange("p (c t) -> p c t", t=C)
                    S.op("dve", lambda e: e.tensor_tensor(out=kdb.rearrange("p (c t) -> p c t", t=C),
                                                          in0=lfb.rearrange("p (c t) -> p c t", t=C),
                                                          in1=ep3[:, :, C - 1:C].to_broadcast([128, nch, C]), op=ALU.mult),
                         reads=["t32_2", "t32_4"], writes=["t16_2"])
                    proj(pP[:, 0, 0:n], si, ki, hT, ["hT"], c0, n, "pP0")
                    S.op("act", lambda e: e.activation(out=vTb, in_=pP[:, 0, 0:n], func=AF.Copy), reads=["pP0"], writes=["t16_3"])
                    g32 = t32[:, 3, 0:n]
                    proj(pP[:, 1, 0:n], sg_, kg, hT, ["hT"], c0, n, "pP1")
                    S.op("act", lambda e: e.activation(out=g32, in_=pP[:, 1, 0:n], func=AF.Silu), reads=["pP1"], writes=["t32_3"])
                    def _pe_prep_fns(ci):
                        a0 = ci * C
                        pb = ci % 2
                        ptk = pTb[0:C, pb * 256:pb * 256 + 128]
                        ptv = pTb[0:C, pb * 256 + 128:pb * 256 + 256]
                        pA = pM[0:C, 256 + pb * 32:256 + pb * 32 + C]
                        return [lambda e, a0=a0, ptk=ptk: e.transpose(ptk, kdb[:, a0:a0 + C], ident_b),
                                lambda e, a0=a0, ptv=ptv: e.transpose(ptv, vTb[:, a0:a0 + C], ident_b),
                                lambda e, a0=a0, pA=pA: e.matmul(pA, lhsT=khb[:, a0:a0 + C], rhs=qt[:, a0:a0 + C], start=True, stop=True)]

                    def _post_prep(ci):
                        pb = ci % 2
                        pA = pM[0:C, 256 + pb * 32:256 + pb * 32 + C]
                        S.op("act", (lambda e, pb=pb: e.activation(out=kvtm[0:C, pb, :, :].rearrange("p a b -> p (a b)"),
                                                                    in_=pTb[0:C, pb * 256:pb * 256 + 256], func=AF.Copy)),
                             reads=["pTb%d" % pb], writes=["kvtm%d" % pb])
                        S.op("dve", (lambda e, pA=pA, pb=pb: e.tensor_tensor(out=A_sb[0:C, pb, 0:C], in0=pA, in1=msk[0:C, 3, 0:C], op=ALU.mult)),
                             reads=["pA%d" % pb, "msk"], writes=["A_sb%d" % pb])

                    def _iter(ci):
                        a0 = ci * C
                        pb = ci % 2
                        nb = (ci + 1) % 2
                        if smp:
                            s_idx = ci
                            s_in, s_out = Ssmp[:, pb, :], Ssmo[:, pb, :]
                            k_in, k_out = "Ssmp%d" % pb, "Ssmo%d" % pb
                            S.op("sp", (lambda e, s_idx=s_idx, s_in=s_in: e.dma_start(out=s_in, in_=st_in[l, s_idx, h])), writes=[k_in])
                        else:
                            bufs = [(Sst[:, l, h, :], "S_%d_%d" % (l, h)), (Stmp[:, :], "Stmp")]
                            (s_in, k_in), (s_out, k_out) = bufs[ci % 2], bufs[(ci + 1) % 2]
                        pS = pM[:, pb * 128:(pb + 1) * 128]
                        fns = []
                        rd = ["kvtm%d" % pb]
                        wr_ = ["pS%d" % pb]
                        if ci + 1 < nch:
                            fns += _pe_prep_fns(ci + 1)
                            rd += ["t16_0", "t16_1", "t16_2", "t16_3", "cstb"]
                            wr_ += ["pTb%d" % nb, "pA%d" % nb]
                        fns.append(lambda e, pb=pb, pS=pS: e.matmul(pS, lhsT=kvtm[0:C, pb, 0, :], rhs=kvtm[0:C, pb, 1, :], start=True, stop=True))
                        S.pe(fns, reads=rd, writes=wr_)
                        if ci + 1 < nch:
                            _post_prep(ci + 1)
                        S.pe([lambda e, a0=a0, pb=pb: e.matmul(pO[:, a0:a0 + C], lhsT=kvtm[0:C, pb, 1, :], rhs=A_sb[0:C, pb, 0:C],
                                                               start=True, stop=False),
                              lambda e, a0=a0, s_in=s_in: e.matmul(pO[:, a0:a0 + C], lhsT=s_in, rhs=q32[:, a0:a0 + C], start=False, stop=True)],
                             reads=["kvtm%d" % pb, "A_sb%d" % pb, k_in, "t32_0"], writes=["pO"])
                        S.op("dve", (lambda e, s_in=s_in, s_out=s_out, pS=pS, a0=a0: e.scalar_tensor_tensor(
                            out=s_out, in0=s_in, scalar=epb[:, a0 + C - 1:a0 + C], in1=pS, op0=ALU.mult, op1=ALU.add)),
                            reads=["pS%d" % pb, "t32_4", k_in], writes=[k_out])
                        if smp:
                            S.op("sp", (lambda e, s_idx=s_idx, s_out=s_out: e.dma_start(out=sts_d[l, s_idx, h], in_=s_out)),
                                 reads=[k_out])
                    S.pe(_pe_prep_fns(0), reads=["t16_0", "t16_1", "t16_2", "t16_3", "cstb"], writes=["pTb0", "pA0"])
                    _post_prep(0)
                    for ci in range(nch):
                        _iter(ci)
                    osq = sqb[:, 0, 0:n]
                    S.op("act", lambda e: e.activation(out=osq, in_=pO[:, 0:n], func=AF.Square), reads=["pO"], writes=["sqb0"])
                    S.pe([lambda e: e.matmul(pN[:, 0:n], lhsT=ones_b, rhs=osq, start=True, stop=True)], reads=["sqb0", "cstb"], writes=["pN"])
                    rstd_from(pN[:, 0:n], n, 1.0 / 128, "pN")
                    S.op("dve", lambda e: e.scalar_tensor_tensor(out=q32, in0=pO[:, 0:n], scalar=goc, in1=rstd[:, 0:n],
                                                                 op0=ALU.mult, op1=ALU.mult),
                         reads=["pO", "rstd", "colsT"], writes=["t32_0"])
                    S.op("dve", lambda e: e.tensor_tensor(out=ogT[:, h * TW + c0:h * TW + c0 + n], in0=q32, in1=g32, op=ALU.mult),
                         reads=["t32_0", "t32_3"], writes=["ogT"])

        def out_proj(w_dram, groups):
            og3 = ogT[:].rearrange("p (a b) -> p a b", b=TW)
            for fo in range(16):
                so, ko = load_w(w_dram[fo])
                for gi, (c0, n) in enumerate(groups):
                    pb = (fo * 3 + gi) % 2
                    proj(pP[:, pb, 0:n], so, ko, og3, ["ogT"], c0, n, "pP%d" % pb)
                    S.op("dve", (lambda e, pb=pb, fo=fo, c0=c0, n=n: e.tensor_tensor(out=xT[:, fo, c0:c0 + n], in0=xT[:, fo, c0:c0 + n],
                                                                                      in1=pP[:, pb, 0:n], op=ALU.add)),
                         reads=["pP%d" % pb, "xT"], writes=["xT"])

        def ffn_layer(l, groups):
            rmsnorm_to_h(C_FN + 16 * l, groups)
            ng = len(groups)
            def hid(fb, gi, j, n):
                o = ((fb * 3 + gi) * 4 + j) * 512
                return ogT[:, o:o + n]
            for fc in range(cfg.get("nfc", NFC)):
                fb = fc % 2
                for j in range(4):
                    f = fc * 4 + j
                    sg_, kg = load_w(wgate_d[l, f])
                    su, ku = load_w(wup_d[l, f])
                    for gi, (c0, n) in enumerate(groups):
                        pb = (j * 3 + gi) % 2
                        proj(pP[:, pb, 0:n], sg_, kg, hT, ["hT"], c0, n, "pP%d" % pb)
                        proj(pU[:, pb, 0:n], su, ku, hT, ["hT"], c0, n, "pU%d" % pb)
                        gs = t32[:, pb, 0:n]
                        S.op("act", (lambda e, gs=gs, pb=pb, n=n: e.activation(out=gs, in_=pP[:, pb, 0:n], func=AF.Silu)),
                             reads=["pP%d" % pb], writes=["t32_%d" % pb])
                        S.op("dve", (lambda e, gs=gs, pb=pb, n=n, fb=fb, gi=gi, j=j: e.tensor_tensor(
                            out=hid(fb, gi, j, n), in0=gs, in1=pU[:, pb, 0:n], op=ALU.mult)),
                            reads=["t32_%d" % pb, "pU%d" % pb], writes=["hid%d_%d" % (fb, gi)])
                sd, kd = load_w(wdown_d[l, fc], nslots=4)
                for fo in range(16):
                    for gi, (c0, n) in enumerate(groups):
                        pb = (fo * 3 + gi) % 2
                        fns = []
                        for j in range(4):
                            fns.append(lambda e, j=j, pb=pb, fo=fo, gi=gi, n=n, sd=sd, fb=fb: e.matmul(
                                pP[:, pb, 0:n], lhsT=wr[:, sd + j, fo * 128:(fo + 1) * 128], rhs=hid(fb, gi, j, n),
                                start=(j == 0), stop=(j == 3)))
                        S.pe(fns, reads=kd + ["hid%d_%d" % (fb, gi)], writes=["pP%d" % pb])
                        S.op("dve", (lambda e, pb=pb, fo=fo, c0=c0, n=n: e.tensor_tensor(out=xT[:, fo, c0:c0 + n], in0=xT[:, fo, c0:c0 + n],
                                                                                          in1=pP[:, pb, 0:n], op=ALU.add)),
                             reads=["pP%d" % pb, "xT"], writes=["xT"])

        def aview(off_words, nwords, dt):
            v = arena[:, off_words:off_words + nwords]
            return v.bitcast(dt) if dt != F32 else v
        KW = 128 + TW
        kT = aview(0, 2368, BF16).rearrange("p (a b) -> p a b", b=KW)
        vaug = aview(2368, 2340, BF16).rearrange("p (a b c) -> p a b c", b=8, c=65)
        vsn = aview(4708, 1040, BF16).rearrange("p (a b c) -> p a b c", b=8, c=65)
        kck = aview(5748, 256, BF16).rearrange("p (a b) -> p a b", b=128)
        vck = aview(6004, 260, BF16).rearrange("p (a b c) -> p a b c", b=2, c=65)
        otm = aview(6264, 64, BF16)
        pTs = aview(6328, 256, BF16).rearrange("p (a b c) -> p a b c", b=2, c=128)
        scs = aview(6584, 512, F32).rearrange("p (a b c) -> p a b c", b=2, c=128)
        qTc = aview(7096, 528, BF16)
        esk = aview(7624, 64, F32)
        skb = aview(7688, 64, F32)
        rcs = aview(7752, 4, F32)

        def rope_apply(src, srckey, n, c0, halo):
            if halo:
                S.op("sp", lambda e: e.dma_start(out=t32[:, 4, 0:n], in_=rope_d[2][:, 0:n]), writes=["t32_4"])
                S.op("sp", lambda e: e.dma_start(out=t32[:, 5, 0:n], in_=rope_d[3][:, 0:n]), writes=["t32_5"])
            else:
                S.op("sp", lambda e: e.dma_start(out=t32[:, 4, 0:n], in_=rope_d[0][:, c0:c0 + n]), writes=["t32_4"])
                S.op("sp", lambda e: e.dma_start(out=t32[:, 5, 0:n], in_=rope_d[1][:, c0:c0 + n]), writes=["t32_5"])
            S.pe([lambda e: e.matmul(pN[:, 0:n], lhsT=rotP, rhs=src, start=True, stop=True)], reads=[srckey, "cst"], writes=["pN"])
            S.op("dve", lambda e: e.tensor_tensor(out=t32[:, 2, 0:n], in0=pN[:, 0:n], in1=t32[:, 5, 0:n], op=ALU.mult),
                 reads=["pN", "t32_5"], writes=["t32_2"])
            S.op("dve", lambda e: e.tensor_tensor(out=t32[:, 3, 0:n], in0=src, in1=t32[:, 4, 0:n], op=ALU.mult),
                 reads=[srckey, "t32_4"], writes=["t32_3"])
            S.op("dve", lambda e: e.tensor_tensor(out=t32[:, 3, 0:n], in0=t32[:, 3, 0:n], in1=t32[:, 2, 0:n], op=ALU.add),
                 reads=["t32_3", "t32_2"], writes=["t32_3"])

        def normed_rot(pin, pkey, n, c0, halo, gcol):
            sq32 = t32[:, 0, 0:n]
            kn = t32[:, 1, 0:n]
            S.op("act", lambda e: e.activation(out=sq32, in_=pin, func=AF.Square), reads=[pkey], writes=["t32_0"])
            S.pe([lambda e: e.matmul(pN[:, 0:n], lhsT=bd64, rhs=sq32, start=True, stop=True)], reads=["t32_0", "cst"], writes=["pN"])
            rstd_from(pN[:, 0:n], n, 1.0 / 64, "pN")
            S.op("dve", lambda e: e.scalar_tensor_tensor(out=kn, in0=pin, scalar=colsT[:, gcol:gcol + 1], in1=rstd[:, 0:n],
                                                         op0=ALU.mult, op1=ALU.mult),
                 reads=[pkey, "rstd", "colsT"], writes=["t32_1"])
            rope_apply(kn, "t32_1", n, c0, halo)

        def kv_compute(groups, halo):
            kvs = cfg.get("kvs", 99)
            rmsnorm_to_h(C_KVN, groups)
            for m in range(4):
                sk, kk = load_w(wk_d[m])
                for (c0, n) in groups:
                    proj(pP[:, 0, 0:n], sk, kk, hT, ["hT"], c0, n, "pP0")
                    if kvs < 2:
                        continue
                    normed_rot(pP[:, 0, 0:n], "pP0", n, c0, halo, C_KN)
                    if kvs < 3:
                        continue
                    if halo:
                        S.op("act", (lambda e, m=m, n=n: e.activation(out=khalo[:, m, :], in_=t32[:, 3, 0:n], func=AF.Copy)),
                             reads=["t32_3"], writes=["khalo"])
                    else:
                        S.op("act", (lambda e, m=m, c0=c0, n=n: e.activation(out=kT[:, m, 128 + c0:128 + c0 + n], in_=t32[:, 3, 0:n], func=AF.Copy)),
                             reads=["t32_3"], writes=["kT"])
                        if kvs < 4:
                            continue
                        if c0 == 512:
                            S.op("sp", (lambda e, m=m: e.dma_start(out=kwp_d[m], in_=t32[:, 3, 384:512])), reads=["t32_3"])
                        if c0 == TP:
                            S.op("sp", (lambda e, m=m, n=n: e.dma_start(out=kws_new[m], in_=t32[:, 3, 0:n])), reads=["t32_3"])
            for fo in range(4 if kvs >= 5 else 0):
                sv, kv = load_w(wv_d[fo])
                for (c0, n) in groups:
                    proj(pP[:, 1, 0:n], sv, kv, hT, ["hT"], c0, n, "pP1")
                    if halo:
                        S.op("act", (lambda e, fo=fo, n=n: e.activation(out=vhalo[:, fo, :], in_=pP[:, 1, 0:n], func=AF.Copy)),
                             reads=["pP1"], writes=["vhalo"])
                        continue
                    v32 = t32[:, 4, 0:n]
                    vb = t16[:, 0, 0:n]
                    S.op("act", (lambda e, v32=v32, n=n: e.activation(out=v32, in_=pP[:, 1, 0:n], func=AF.Copy)), reads=["pP1"], writes=["t32_4"])
                    S.op("act", (lambda e, vb=vb, v32=v32: e.activation(out=vb, in_=v32, func=AF.Copy)), reads=["t32_4"], writes=["t16_0"])
                    if kvs < 6:
                        continue
                    if c0 == 512:
                        S.op("sp", (lambda e, fo=fo: e.dma_start(out=vwp_d[fo], in_=t32[:, 4, 384:512])), reads=["t32_4"])
                    if c0 == TP:
                        S.op("sp", (lambda e, fo=fo, n=n: e.dma_start(out=vws_new[fo], in_=t32[:, 4, 0:n])), reads=["t32_4"])
                    if kvs < 7:
                        continue
                    if c0 == TP:
                        for s in range(NS):
                            S.pe([lambda e, s=s, vb=vb: e.transpose(pTb[0:TS, 0:128], vb[:, s * TS:(s + 1) * TS], ident_b)],
                                 reads=["t16_0", "cstb"], writes=["pTb0"])
                            S.op("act", (lambda e, s=s, fo=fo: e.activation(out=vsn[0:TS, s, 2 * fo:2 * fo + 2, 0:64],
                                                                             in_=pTb[0:TS, 0:128].rearrange("p (a b) -> p a b", b=64), func=AF.Copy)),
                                 reads=["pTb0"], writes=["vsn"])
                    else:
                        for bi in range(n // 128):
                            kb = 1 + (c0 + bi * 128) // 128
                            S.pe([lambda e, bi=bi, vb=vb: e.transpose(pTb[:, 0:128], vb[:, bi * 128:(bi + 1) * 128], ident_b)],
                                 reads=["t16_0", "cstb"], writes=["pTb0"])
                            S.op("act", (lambda e, kb=kb, fo=fo: e.activation(out=vaug[:, kb, 2 * fo:2 * fo + 2, 0:64],
                                                                               in_=pTb[:, 0:128].rearrange("p (a b) -> p a b", b=64), func=AF.Copy)),
                                 reads=["pTb0"], writes=["vaug"])

        def load_cache(m):
            for s in range(NS):
                ctm = t32[:, 0:2, 0:128]
                S.op("sp", (lambda e, s=s: e.dma_start(out=t32[:, 0, 0:128], in_=ck_d[s][:, m * 128:(m + 1) * 128])), writes=["t32_0"])
                S.op("sp", (lambda e, s=s: e.dma_start(out=t32[:, 1, 0:128], in_=cv_d[s][:, m * 128:(m + 1) * 128])), writes=["t32_1"])
                S.op("act", (lambda e, s=s: e.activation(out=vck[:, s, :, 0:64], in_=t32[:, 1, 0:128].rearrange("p (g d) -> p g d", d=64), func=AF.Copy)),
                     reads=["t32_1"], writes=["vck"])
                S.op("act", (lambda e: e.activation(out=t16[:, 1, 0:128], in_=t32[:, 0, 0:128], func=AF.Copy)), reads=["t32_0"], writes=["t16_1"])
                S.pe([lambda e: e.transpose(pTb[:, 256:384], t16[:, 1, 0:128], ident_b)], reads=["t16_1", "cstb"], writes=["pTb1"])
                S.op("act", (lambda e, s=s: e.activation(out=kck[:, s, :], in_=pTb[:, 256:384], func=AF.Copy)), reads=["pTb1"], writes=["kck"])

        def attn_layer(j, groups):
            rmsnorm_to_h(C_AN + 16 * j, groups)
            og3 = ogT[:].rearrange("p (a b) -> p a b", b=TW)
            it = [0]

            def attend(nq, qc0, fo, m, prevK, prevV, curK, curV, nk, mprev, mcur, pkeys):
                for hh in range(2):
                    half = hh * 64
                    g = 2 * m + hh
                    hq = 8 * m + 4 * hh + (fo % 4)
                    pb = it[0] % 2
                    it[0] += 1
                    qsl = qTc[half:half + 64, qc0:qc0 + nq]
                    sc = scs[:, pb]
                    pT_ = pTs[:, pb]
                    pS0 = pU[:, pb, 0:nq]
                    pS1 = pU[0:nk, pb, 128:128 + nq]
                    S.pe([lambda e, half=half, qsl=qsl, pS0=pS0: e.matmul(pS0, lhsT=prevK(half), rhs=qsl, start=True, stop=True),
                          lambda e, half=half, qsl=qsl, pS1=pS1: e.matmul(pS1, lhsT=curK(half), rhs=qsl, start=True, stop=True)],
                         reads=["qTc", "kT"] + pkeys, writes=["pU%d" % pb])
                    S.op("dve", (lambda e, sc=sc, pS0=pS0: e.scalar_tensor_tensor(out=sc[:, 0, 0:nq], in0=pS0, scalar=ASC, in1=mprev[:, 0:nq],
                                                                                   op0=ALU.mult, op1=ALU.add)),
                         reads=["pU%d" % pb, "msk"], writes=["sc%d" % pb])
                    S.op("dve", (lambda e, sc=sc, pS1=pS1: e.scalar_tensor_tensor(out=sc[0:nk, 1, 0:nq], in0=pS1, scalar=ASC, in1=mcur[0:nk, 0:nq],
                                                                                   op0=ALU.mult, op1=ALU.add)),
                         reads=["pU%d" % pb, "msk"], writes=["sc%d" % pb])
                    S.op("act", (lambda e, sc=sc, pT_=pT_: e.activation(out=pT_[:, 0, 0:nq], in_=sc[:, 0, 0:nq], func=AF.Exp)),
                         reads=["sc%d" % pb], writes=["pT%d" % pb])
                    S.op("act", (lambda e, sc=sc, pT_=pT_: e.activation(out=pT_[0:nk, 1, 0:nq], in_=sc[0:nk, 1, 0:nq], func=AF.Exp)),
                         reads=["sc%d" % pb], writes=["pT%d" % pb])
                    po = pO[0:nq, pb * 128:pb * 128 + 65]
                    S.pe([lambda e, g=g, pT_=pT_, po=po: e.matmul(po, lhsT=pT_[:, 0, 0:nq], rhs=prevV(g), start=True, stop=False),
                          lambda e, g=g, pT_=pT_, po=po: e.matmul(po, lhsT=pT_[0:nk, 1, 0:nq], rhs=curV(g), start=False, stop=True)],
                         reads=["pT%d" % pb, "vaug", "vsn"] + pkeys, writes=["pO%d" % pb])
                    rc = rcs[0:nq, pb:pb + 1]
                    S.op("dve", (lambda e, rc=rc, po=po, hq=hq: e.tensor_tensor(out=rc, in0=po[:, 64:65], in1=esk[0:nq, j * 32 + hq:j * 32 + hq + 1],
                                                                                 op=ALU.add)),
                         reads=["pO%d" % pb, "esk"], writes=["rc%d" % pb])
                    S.op("dve", (lambda e, rc=rc: e.reciprocal(out=rc, in_=rc)), reads=["rc%d" % pb], writes=["rc%d" % pb])
                    S.op("dve", (lambda e, rc=rc, po=po, half=half: e.tensor_scalar(out=otm[0:nq, half:half + 64], in0=po[:, 0:64], scalar1=rc,
                                                                                     scalar2=None, op0=ALU.mult)),
                         reads=["pO%d" % pb, "rc%d" % pb], writes=["otm"])
                S.pe([lambda e: e.transpose(pTb[:, 512:512 + nq], otm[0:nq, :], ident_b[0:nq, 0:nq])], reads=["otm", "cstb"], writes=["pTb2"])
                S.op("act", (lambda e: e.activation(out=og3[:, fo, qc0:qc0 + nq], in_=pTb[:, 512:512 + nq], func=AF.Copy)),
                     reads=["pTb2"], writes=["ogT"])

            for m in range(cfg.get("mstart", 0), cfg.get("nm", 4)):
                if cfg.get("astop", 99) >= 5:
                    load_cache(m)
                for i in range(4 if cfg.get("astop", 99) >= 6 else 0):
                    fo = 4 * m + i
                    sq_, kq = load_w(awq_d[j, fo])
                    for (c0, n) in groups:
                        proj(pP[:, 0, 0:n], sq_, kq, hT, ["hT"], c0, n, "pP0")
                        normed_rot(pP[:, 0, 0:n], "pP0", n, c0, False, C_QN + j)
                        S.op("act", (lambda e, c0=c0, n=n: e.activation(out=qTc[:, c0:c0 + n], in_=t32[:, 3, 0:n], func=AF.Copy)),
                             reads=["t32_3"], writes=["qTc"])
                    for qb in range(8 if cfg.get("astop", 99) >= 7 else 0):
                        k0 = qb * 128
                        attend(128, qb * 128, fo, m,
                               lambda half, k0=k0, m=m: kT[half:half + 64, m, k0:k0 + 128],
                               lambda g, qb=qb: vaug[:, qb, g, :],
                               lambda half, k0=k0, m=m: kT[half:half + 64, m, k0 + 128:k0 + 256],
                               lambda g, qb=qb: vaug[:, qb + 1, g, :],
                               128, msk[:, 2, :] if qb == 0 else msk[:, 0, :], msk[:, 1, :], [])
                    for s in range(NS if cfg.get("astop", 99) >= 8 else 0):
                        c1 = 128 + TP + s * TS
                        attend(TS, TP + s * TS, fo, m,
                               lambda half, s=s: kck[half:half + 64, s, :],
                               lambda g, s=s: vck[:, s, g % 2, :],
                               lambda half, c1=c1, m=m: kT[half:half + 64, m, c1:c1 + TS],
                               lambda g, s=s: vsn[0:TS, s, g, :],
                               TS, msk[:, 0, :], msk[:, 1, :], ["kck", "vck"])
            if cfg.get("astop", 99) >= 9:
                out_proj(awo_d[j], groups)

        xs3 = xs_d.rearrange("p (a b) -> p a b", b=NS * TS)
        for t in range(NT - n_tiles, NT):
            last = (t == NT - 1)
            groups = list(groups_main) + ([(TP, NS * TS)] if last else [])
            S.op("sp", (lambda e, t=t: e.dma_start(out=xT[:, :, 0:TP], in_=xin[t].rearrange("p (a b) -> p a b", b=TP))),
                 writes=["xT"], reads=[])
            if last:
                S.op("sp", lambda e: e.dma_start(out=xT[:, :, TP:TW], in_=xs3), writes=["xT"])
            if t == NT - n_tiles:
                S.op("dve", lambda e: e.memset(arena[:, 0:4096], 0.0),
                     writes=["S_%d_%d" % (l, h) for l in range(2) for h in range(16)])
                S.op("dve", lambda e: e.memset(arena[:, 7760:8272], 0.0), writes=["khalo", "vhalo"])
            for l in range(cfg.get("nl", 2)):
                if cfg.get("hg", True):
                    hgrn_layer(l, groups, last)
                if cfg.get("ffn", True):
                    ffn_layer(l, groups)
            if do_attn and t == NT - 2:
                kv_compute([(TP - 128, 128)], True)
        if do_attn:
            last_groups = list(groups_main) + [(TP, NS * TS)]
            S.barrier()
            S.op("sp", lambda e: e.dma_start(out=skb, in_=cols_d[:, C_SK:C_SK + 64]), writes=["skb"])
            S.op("act", lambda e: e.activation(out=esk, in_=skb, func=AF.Exp), reads=["skb"], writes=["esk"])
            S.op("dve", lambda e: e.memset(vaug[:, :, :, 64:65], 1.0), writes=["vaug"])
            S.op("dve", lambda e: e.memset(vsn[:, :, :, 64:65], 1.0), writes=["vsn"])
            S.op("dve", lambda e: e.memset(vck[:, :, :, 64:65], 1.0), writes=["vck"])
            S.op("act", lambda e: e.activation(out=kT[:, :, 0:128], in_=khalo, func=AF.Copy), reads=["khalo"], writes=["kT"])
            for fo in range(4):
                S.pe([lambda e, fo=fo: e.transpose(pTb[:, 0:128], vhalo[:, fo, :], ident_b)], reads=["vhalo", "cstb"], writes=["pTb0"])
                S.op("act", (lambda e, fo=fo: e.activation(out=vaug[:, 0, 2 * fo:2 * fo + 2, 0:64],
                                                            in_=pTb[:, 0:128].rearrange("p (a b) -> p a b", b=64), func=AF.Copy)),
                     reads=["pTb0"], writes=["vaug"])
            if cfg.get("astop", 99) >= 2:
                kv_compute(last_groups, False)
            if cfg.get("astop", 99) >= 3:
                for s in range(NS):
                    S.op("sp", (lambda e, s=s: e.dma_start(out=kws_old[s], in_=ck_d[s, TS:128, :])))
                    S.op("sp", (lambda e, s=s: e.dma_start(out=vws_old[s], in_=cv_d[s, TS:128, :])))
            for j in range(cfg.get("nl", 2) if cfg.get("astop", 99) >= 4 else 0):
                attn_layer(j, last_groups)
                if cfg.get("ffn", True):
                    ffn_layer(2 + j, last_groups)
        S.op("sp", lambda e: e.dma_start(out=yT_d.rearrange("p (a b) -> p a b", b=TW), in_=xT[:]), reads=["xT"])
        S.finish()
        with nc.Block() as block:
            S.emit(block)
    return nc


_NC_CACHE = {}


def _relayout_w(w, ncol_chunks):
    K, F = w.shape
    a = w.reshape(K // 128, 128, F // 128, 128)
    return np.ascontiguousarray(a.transpose(2, 1, 0, 3)).reshape(F // 128, 128, (K // 128) * 128)


def kernel(x_prompt, x_sample, state_hgrn, cache_k_win, cache_v_win,
           hgrn_norm, hgrn_wq, hgrn_wf, hgrn_wi, hgrn_wg, hgrn_lb_logits, hgrn_onorm, hgrn_wo,
           kv_norm, w_k, w_v, k_norm,
           attn_norm, attn_wq, q_norm, sinks, attn_wo,
           ffn_norm, w_gate, w_up, w_down, _cfg=None):
    f = lambda a: np.asarray(a, dtype=np.float32)
    x_prompt, x_sample, state_hgrn = f(x_prompt), f(x_sample), f(state_hgrn)
    cache_k_win, cache_v_win = f(cache_k_win), f(cache_v_win)
    ncores = (_cfg or {}).get("ncores", 8)
    shared = {}
    for nm, w in (("q", hgrn_wq), ("f", hgrn_wf), ("i", hgrn_wi), ("g", hgrn_wg), ("o", hgrn_wo)):
        w = f(w)
        shared["hw_" + nm] = np.stack([_relayout_w(w[l], 16) for l in range(2)])
    shared["wgate"] = np.stack([_relayout_w(f(w_gate)[l], NF) for l in range(4)])
    shared["wup"] = np.stack([_relayout_w(f(w_up)[l], NF) for l in range(4)])
    wd = f(w_down)
    shared["wdown"] = np.ascontiguousarray(wd.reshape(4, NFC, 4, 128, 2048))
    perm = []
    for fo in range(16):
        m, i = fo // 4, fo % 4
        for hq in (8 * m + i, 8 * m + 4 + i):
            perm.extend(range(hq * 64, hq * 64 + 64))
    perm = np.array(perm)
    awq = f(attn_wq)[:, :, perm]
    awo = f(attn_wo)[:, perm, :]
    shared["awq"] = np.stack([_relayout_w(awq[j], 16) for j in range(2)])
    shared["awo"] = np.stack([_relayout_w(awo[j], 16) for j in range(2)])
    shared["wk"] = _relayout_w(f(w_k), 4)
    shared["wv"] = _relayout_w(f(w_v), 4)

    def colify(v):
        return f(v).reshape(16, 128).T

    cols = np.zeros((128, 320), np.float32)
    hn, an, fn_ = f(hgrn_norm), f(attn_norm), f(ffn_norm)
    for l in range(2):
        cols[:, 0 + 16 * l:16 + 16 * l] = colify(hn[l])
        cols[:, 32 + 16 * l:48 + 16 * l] = colify(an[l])
    for l in range(4):
        cols[:, 64 + 16 * l:80 + 16 * l] = colify(fn_[l])
    cols[:, 128:144] = colify(kv_norm)
    cols[:, 240:242] = f(hgrn_onorm).T
    cols[:, 242] = np.tile(f(k_norm), 2)
    cols[:, 243:245] = np.tile(f(q_norm), (1, 2)).T
    cols[:, 256:320] = np.broadcast_to(f(sinks).reshape(1, 64), (128, 64))
    shared["cols_base"] = cols
    shared["lb_logits"] = np.ascontiguousarray(f(hgrn_lb_logits).reshape(2, 16, 128).transpose(2, 0, 1)).reshape(128, 32)

    cst = np.zeros((4, 128, 128), np.float32)
    cst[0] = np.eye(128)
    cst[1] = 1.0
    cst[2, :64, :64] = 1.0
    cst[2, 64:, 64:] = 1.0
    for mm in range(128):
        b, o = (mm // 64) * 64, mm % 64
        if o < 32:
            cst[3, b + o + 32, mm] = -1.0
        else:
            cst[3, b + o - 32, mm] = 1.0
    kk = np.arange(128)[:, None]
    qq = np.arange(128)[None, :]
    masks = np.zeros((4, 128, 128), np.float32)
    masks[0] = np.where(kk > qq, 0.0, NEG)
    masks[1] = np.where(kk <= qq, 0.0, NEG)
    masks[3, :32, :32] = (np.arange(32)[:, None] <= np.arange(32)[None, :]).astype(np.float32)
    scm = np.ones((128, 544), np.float32)
    scm[:, 0:512:32] = 0.0
    scm[:, 512:544:8] = 0.0
    half = 32
    inv = (10000.0 ** (-np.arange(half, dtype=np.float32) / half)).astype(np.float32)
    invp = inv[np.arange(128) % 32][:, None]

    in_maps = []
    for c in range(ncores):
        seq, r = c // 4, c % 4
        nreal = (r + 1) * TP
        xp = np.zeros((SEQ, D), np.float32)
        xp[SEQ - nreal:] = x_prompt[seq, :nreal]
        xin = np.ascontiguousarray(xp.reshape(NT, TP, 16, 128).transpose(0, 3, 2, 1)).reshape(NT, 128, 16 * TP)
        xsm = x_sample[NS * c:NS * (c + 1)].reshape(NS * TS, 16, 128)
        xs = np.ascontiguousarray(xsm.transpose(2, 1, 0)).reshape(128, 16 * NS * TS)
        pos = np.concatenate([r * TP + np.arange(TP), np.tile(PAST + np.arange(TS), NS)]).astype(np.float32)
        posh = (r * TP - 128 + np.arange(128)).astype(np.float32)
        rope = np.zeros((4, 128, TW), np.float32)
        ang = invp * pos[None, :]
        rope[0], rope[1] = np.cos(ang), np.sin(ang)
        angh = invp * posh[None, :]
        rope[2, :, :128], rope[3, :, :128] = np.cos(angh), np.sin(angh)
        mk = masks.copy()
        mk[2] = masks[0] if r > 0 else NEG
        m = dict(shared)
        m.pop("cols_base"); m.pop("lb_logits")
        m.update(xin=xin, xs=xs,
                 st_in=np.ascontiguousarray(state_hgrn[:, NS * c:NS * (c + 1)]),
                 ck=np.ascontiguousarray(cache_k_win[NS * c:NS * (c + 1)].reshape(NS, 128, 512)),
                 cv=np.ascontiguousarray(cache_v_win[NS * c:NS * (c + 1)].reshape(NS, 128, 512)),
                 cols=cols, lbl=shared["lb_logits"], rope=rope, masks=mk, cst=cst, scm=scm)
        in_maps.append(m)

    key = repr(_cfg)
    if key not in _NC_CACHE:
        _NC_CACHE[key] = build_nc(_cfg)
    nc = _NC_CACHE[key]
    res = run_bass_kernel_spmd(nc, in_maps, core_ids=list(range(ncores)), trace=bool((_cfg or {}).get("trace", False)))
    if (_cfg or {}).get("trace"):
        print("EXEC_TIME_NS", res.exec_time_ns)
    R = res.results
    global _LAST_R
    _LAST_R = R
    y_prompt = np.zeros((2, SEQ, D), np.float32)
    y_sample = np.zeros((32, TS, D), np.float32)
    st_p = np.zeros((2, 2, 16, 128, 128), np.float32)
    st_s = np.zeros((2, 32, 16, 128, 128), np.float32)
    kw_p = np.zeros((2, 128, 8, 64), np.float32)
    vw_p = np.zeros((2, 128, 8, 64), np.float32)
    kw_s = np.zeros((32, 128, 8, 64), np.float32)
    vw_s = np.zeros((32, 128, 8, 64), np.float32)
    for c in range(ncores):
        seq, r = c // 4, c % 4
        yT = R[c]["yT"].reshape(128, 16, TW)
        ytm = yT.transpose(2, 1, 0).reshape(TW, D)
        y_prompt[seq, r * TP:(r + 1) * TP] = ytm[:TP]
        y_sample[NS * c:NS * (c + 1)] = ytm[TP:].reshape(NS, TS, D)
        st_s[:, NS * c:NS * (c + 1)] = R[c]["sts"]
        knew = R[c]["kws_new"].reshape(512, NS, TS).transpose(1, 2, 0)
        vnew = R[c]["vws_new"].reshape(512, NS, TS).transpose(1, 2, 0)
        kw_s[NS * c:NS * (c + 1), :120] = R[c]["kws_old"].reshape(NS, 120, 8, 64)
        vw_s[NS * c:NS * (c + 1), :120] = R[c]["vws_old"].reshape(NS, 120, 8, 64)
        kw_s[NS * c:NS * (c + 1), 120:] = knew.reshape(NS, TS, 8, 64)
        vw_s[NS * c:NS * (c + 1), 120:] = vnew.reshape(NS, TS, 8, 64)
        if r == 3:
            st_p[:, seq] = R[c]["stp"]
            kw_p[seq] = R[c]["kwp"].reshape(512, 128).T.reshape(128, 8, 64)
            vw_p[seq] = R[c]["vwp"].reshape(512, 128).T.reshape(128, 8, 64)
    return (y_prompt, y_sample, st_p, st_s, kw_p, vw_p, kw_s, vw_s)
```

```python
import numpy as np
from contextlib import ExitStack
import concourse.bass as bass
import concourse.mybir as mybir
from concourse.bass_utils import run_bass_kernel_spmd

F32, BF16 = mybir.dt.float32, mybir.dt.bfloat16
AF = mybir.ActivationFunctionType
ALU = mybir.AluOpType

D = 2048
NKC = 16
DFF = 5632
NF = 44
NFC = 11
SEQ = 4096
TP = 1024
NS = 4
TS = 8
TW = TP + NS * TS
NT = 4
PAST = 16384
EPS = 1e-6
LN_MINF = float(np.log(1e-30))
QSC = 128 ** -0.5
ASC = 64 ** -0.5
NEG = -1e30


class Sched:
    def __init__(self, nc, es):
        self.nc = nc
        self.es = es
        self.E = {}
        for n in ("pe", "act", "dve", "sp", "gq"):
            self.E[n] = dict(name=n, recs=[], seen={}, dma=n in ("sp", "gq"), pend=[])
        for n in ("pe", "act", "dve"):
            e = self.E[n]
            e["sem"] = es.enter_context(nc.semaphore(n + "_s0"))
            e["cnt"] = 0
            e["ep"] = 0
        for n in ("sp", "gq"):
            e = self.E[n]
            e["sems"] = [es.enter_context(nc.semaphore("%s_d%d" % (n, i))) for i in range(8)]
            e["tot"] = [0] * 8
            e["n"] = 0
        self.bw = {}
        self.br = {}
        self.last = {}

    def _deps(self, eng, reads, writes):
        toks = []
        for k in reads:
            if k in self.bw:
                toks.append(self.bw[k])
        for k in writes:
            if k in self.bw:
                toks.append(self.bw[k])
            toks.extend(self.br.get(k, ()))
        need = {}
        for (sem, val, src) in toks:
            if src == "pe" and eng["name"] == "pe":
                continue
            key = id(sem)
            if key not in need or need[key][1] < val:
                need[key] = (sem, val)
        waits = list(eng["pend"])
        eng["pend"] = []
        for key, (sem, val) in need.items():
            if eng["seen"].get(key, 0) < val:
                eng["seen"][key] = val
                waits.append((sem, val))
        return waits

    def op(self, en, fn, reads=(), writes=(), inc=True):
        eng = self.E[en]
        waits = self._deps(eng, reads, writes)
        if eng["dma"]:
            i = eng["n"] % len(eng["sems"])
            eng["n"] += 1
            sem = eng["sems"][i]
            if eng["tot"][i] > 0 and eng["seen"].get(id(sem), 0) < eng["tot"][i]:
                eng["seen"][id(sem)] = eng["tot"][i]
                waits.append((sem, eng["tot"][i]))
            eng["tot"][i] += 16
            tok = (sem, eng["tot"][i], en)
            eng["recs"].append((waits, fn, sem, 16))
        elif inc:
            if eng["cnt"] >= 30000:
                eng["ep"] += 1
                eng["sem"] = self.es.enter_context(self.nc.semaphore("%s_s%d" % (en, eng["ep"])))
                eng["cnt"] = 0
            eng["cnt"] += 1
            tok = (eng["sem"], eng["cnt"], en)
            eng["recs"].append((waits, fn, eng["sem"], 1))
        else:
            eng["recs"].append((waits, fn, None, 0))
            return None
        self.last[en + str(id(tok[0]))] = tok
        for k in reads:
            self.br.setdefault(k, []).append(tok)
        for k in writes:
            self.bw[k] = tok
            self.br[k] = []
        return tok

    def pe(self, fns, reads, writes):
        for f in fns[:-1]:
            self.op("pe", f, reads, writes, inc=False)
        return self.op("pe", fns[-1], reads, writes, inc=True)

    def barrier(self):
        allw = [(t[0], t[1]) for t in self.last.values()]
        for e in self.E.values():
            e["pend"] = list(allw)
            for (sem, val) in allw:
                e["seen"][id(sem)] = max(e["seen"].get(id(sem), 0), val)

    def finish(self):
        eng = self.E["sp"]
        waits = []
        for tok in self.last.values():
            waits.append((tok[0], tok[1]))
        eng["recs"].append((waits, None, None, 0))

    def emit(self, block):
        def replay(recs):
            def run(e):
                for (waits, fn, sem, amt) in recs:
                    for (s, v) in waits:
                        e.wait_ge(s, v)
                    if fn is not None:
                        ins = fn(e)
                        if sem is not None:
                            ins.then_inc(sem, amt)
            return run
        block.sync(replay(self.E["sp"]["recs"]))
        block.gpsimd(replay(self.E["gq"]["recs"]))
        block.tensor(replay(self.E["pe"]["recs"]))
        block.scalar(replay(self.E["act"]["recs"]))
        block.vector(replay(self.E["dve"]["recs"]))


def build_nc(cfg=None):
    cfg = cfg or {}
    n_tiles = cfg.get("n_tiles", NT)
    do_attn = cfg.get("attn", True)
    nc = bass.Bass("TRN2", target_bir_lowering=False, dynamic_dma_scratch_size=4096)
    es = ExitStack()
    with es:
        def din(name, shape):
            return nc.dram_tensor(name, list(shape), F32, kind="ExternalInput").ap()

        def dout(name, shape):
            return nc.dram_tensor(name, list(shape), F32, kind="ExternalOutput").ap()

        def sb(name, shape, dt):
            return es.enter_context(nc.sbuf_tensor(name, list(shape), dt))

        def ps(name, shape, dt):
            return es.enter_context(nc.psum_tensor(name, list(shape), dt))

        xin = din("xin", [NT, 128, NKC * TP])
        xs_d = din("xs", [128, NKC * NS * TS])
        st_in = din("st_in", [2, NS, 16, 128, 128])
        ck_d = din("ck", [NS, 128, 512])
        cv_d = din("cv", [NS, 128, 512])
        hw_d = {k: din("hw_" + k, [2, 16, 128, 2048]) for k in ("q", "f", "i", "g", "o")}
        wgate_d = din("wgate", [4, NF, 128, 2048])
        wup_d = din("wup", [4, NF, 128, 2048])
        wdown_d = din("wdown", [4, NFC, 4, 128, 2048])
        awq_d = din("awq", [2, 16, 128, 2048])
        awo_d = din("awo", [2, 16, 128, 2048])
        wk_d = din("wk", [4, 128, 2048])
        wv_d = din("wv", [4, 128, 2048])
        cols_d = din("cols", [128, 320])
        rope_d = din("rope", [4, 128, TW])
        mask_d = din("masks", [4, 128, 128])
        cst_d = din("cst", [4, 128, 128])
        scm_d = din("scm", [128, 544])
        lbl_d = din("lbl", [128, 32])

        yT_d = dout("yT", [128, NKC * TW])
        stp_d = dout("stp", [2, 16, 128, 128])
        sts_d = dout("sts", [2, NS, 16, 128, 128])
        kwp_d = dout("kwp", [4, 128, 128])
        vwp_d = dout("vwp", [4, 128, 128])
        kws_old = dout("kws_old", [NS, 120, 512])
        vws_old = dout("vws_old", [NS, 120, 512])
        kws_new = dout("kws_new", [4, 128, NS * TS])
        vws_new = dout("vws_new", [4, 128, NS * TS])

        xT = sb("xT", [128, NKC, TW], F32)
        hT = sb("hT", [128, NKC, TW], BF16)
        ogT = sb("ogT", [128, NKC * TW], BF16)
        arena = sb("arena", [128, 8272], F32)
        Sst = arena[:, 0:4096].rearrange("p (a b c) -> p a b c", b=16, c=128)
        khalo = arena[:, 7760:8016].bitcast(BF16).rearrange("p (a b) -> p a b", b=128)
        vhalo = arena[:, 8016:8272].bitcast(BF16).rearrange("p (a b) -> p a b", b=128)
        wr = sb("wr", [128, 6, 2048], BF16)
        t32 = sb("t32", [128, 6, 512], F32)
        t16 = sb("t16", [128, 4, 512], BF16)
        sqb = sb("sqb", [128, 1, 512], BF16)
        rstd = sb("rstd", [128, 512], F32)
        kvtm = sb("kvtm", [32, 2, 2, 128], BF16)
        A_sb = sb("A_sb", [32, 2, 32], BF16)
        S_bf = sb("S_bf", [128, 2, 128], BF16)
        Ssmp = sb("Ssmp", [128, 2, 128], F32)
        Ssmo = sb("Ssmo", [128, 2, 128], F32)
        Stmp = sb("Stmp", [128, 128], F32)
        colsT = sb("colsT", [128, 320], F32)
        ecol = sb("ecol", [128, 8], F32)
        ecol_l = sb("ecol_l", [128, 32], F32)
        cst = sb("cst_sb", [128, 4, 128], F32)
        cstb = sb("cstb", [128, 2, 128], BF16)
        msk = sb("msk", [128, 4, 128], F32)
        scm = sb("scm_sb", [128, 544], F32)

        pP = ps("pP", [128, 2, 512], F32)
        pO = ps("pO", [128, 512], F32)
        pM = ps("pM", [128, 512], F32)
        pTb = ps("pTb", [128, 1024], BF16)
        pN = ps("pN", [128, 512], F32)
        pU = ps("pU", [128, 2, 512], F32)

        S = Sched(nc, es)
        ident_b = cstb[:, 0, :]
        ones_b = cstb[:, 1, :]
        ident_f = cst[:, 0, :]
        bd64 = cst[:, 2, :]
        rotP = cst[:, 3, :]
        tri = msk[0:32, 3, 0:32]

        C_HN, C_AN, C_FN, C_KVN = 0, 32, 64, 128
        C_LB, C_OMLB, C_NOMLB = 144, 176, 208
        C_GO, C_KN, C_QN, C_SK = 240, 242, 243, 256

        S.op("sp", lambda e: e.dma_start(out=colsT[:], in_=cols_d), writes=["colsT"])
        S.op("sp", lambda e: e.dma_start(out=cst[:], in_=cst_d.rearrange("a p c -> p a c")), writes=["cst"])
        S.op("sp", lambda e: e.dma_start(out=msk[:], in_=mask_d.rearrange("a p c -> p a c")), writes=["msk"])
        S.op("sp", lambda e: e.dma_start(out=scm[:], in_=scm_d), writes=["scm"])
        S.op("act", lambda e: e.activation(out=cstb[:], in_=cst[:, 0:2, :], func=AF.Copy), reads=["cst"], writes=["cstb"])
        S.op("sp", lambda e: e.dma_start(out=ecol_l[:], in_=lbl_d), writes=["ecol_l"])
        S.op("dve", lambda e: e.tensor_tensor(out=ecol_l[:, 0:16], in0=ecol_l[:, 16:32], in1=ecol_l[:, 0:16], op=ALU.subtract),
             reads=["ecol_l"], writes=["ecol_l"])
        S.op("act", lambda e: e.activation(out=colsT[:, C_LB + 16:C_LB + 32], in_=ecol_l[:, 0:16], func=AF.Sigmoid),
             reads=["ecol_l", "colsT"], writes=["colsT"])
        S.op("dve", lambda e: e.tensor_scalar(out=colsT[:, C_OMLB:C_OMLB + 32], in0=colsT[:, C_LB:C_LB + 32], scalar1=-1.0, scalar2=1.0,
                                              op0=ALU.mult, op1=ALU.add), reads=["colsT"], writes=["colsT"])
        S.op("dve", lambda e: e.tensor_scalar(out=colsT[:, C_NOMLB:C_NOMLB + 32], in0=colsT[:, C_OMLB:C_OMLB + 32], scalar1=-1.0, scalar2=None,
                                              op0=ALU.mult), reads=["colsT"], writes=["colsT"])

        groups_main = [(0, 512), (512, 512)]
        wslot = [0]

        def load_w(dram_ap, nslots=1):
            s0 = wslot[0]
            if s0 + nslots > 6:
                s0 = 0
            wslot[0] = (s0 + nslots) % 6
            keys = ["w%d" % (s0 + i) for i in range(nslots)]
            for i in range(nslots):
                src = dram_ap[i] if nslots > 1 else dram_ap
                S.op("gq", (lambda e, i=i, src=src: e.dma_start(out=wr[:, s0 + i, :], in_=src)), writes=[keys[i]])
            return s0, keys

        def proj(pout, wslot_idx, wkeys, src, srckeys, c0, n, outkey):
            fns = []
            for kc in range(NKC):
                fns.append(lambda e, kc=kc: e.matmul(pout, lhsT=wr[:, wslot_idx, kc * 128:(kc + 1) * 128],
                                                     rhs=src[:, kc, c0:c0 + n], start=(kc == 0), stop=(kc == NKC - 1)))
            S.pe(fns, reads=list(wkeys) + list(srckeys), writes=[outkey])

        def rstd_from(psum_ap, n, inv_n, pkey):
            S.op("act", lambda e: e.activation(out=rstd[:, 0:n], in_=psum_ap, func=AF.Ln, scale=inv_n, bias=ecol[:, 0:1]),
                 reads=[pkey, "ecol"], writes=["rstd"])
            S.op("act", lambda e: e.activation(out=rstd[:, 0:n], in_=rstd[:, 0:n], func=AF.Exp, scale=-0.5),
                 reads=["rstd"], writes=["rstd"])

        S.op("dve", lambda e: e.memset(ecol[:, 0:1], EPS), writes=["ecol"])

        def rmsnorm_to_h(gcol0, groups, srcT=None):
            for (c0, n) in groups:
                _rms_group(gcol0, c0, n)

        def _rms_group(gcol0, c0, n):
            if True:
                fns = []
                for kc in range(NKC):
                    b = kc % 4
                    S.op("act", (lambda e, kc=kc, b=b: e.activation(out=t16[:, b, 0:n], in_=xT[:, kc, c0:c0 + n], func=AF.Square)),
                         reads=["xT"], writes=["t16_%d" % b])
                    S.op("pe", (lambda e, kc=kc, b=b: e.matmul(pN[:, 0:n], lhsT=ones_b, rhs=t16[:, b, 0:n],
                                                               start=(kc == 0), stop=(kc == NKC - 1))),
                         reads=["t16_%d" % b, "cstb"], writes=["pN"], inc=True)
                rstd_from(pN[:, 0:n], n, 1.0 / D, "pN")
                for kc in range(NKC):
                    S.op("dve", (lambda e, kc=kc: e.scalar_tensor_tensor(out=hT[:, kc, c0:c0 + n], in0=xT[:, kc, c0:c0 + n],
                                                                         scalar=colsT[:, gcol0 + kc:gcol0 + kc + 1], in1=rstd[:, 0:n],
                                                                         op0=ALU.mult, op1=ALU.mult)),
                         reads=["xT", "rstd", "colsT"], writes=["hT"])

        def hgrn_layer(l, groups, last_tile, full_groups=None):
            if full_groups is None:
                full_groups = groups
            rmsnorm_to_h(C_HN + 16 * l, groups)
            for h in range(cfg.get("nh", 16)):
                sq = kq = sg_ = kg = None
                if full_groups:
                    sq, kq = load_w(hw_d["q"][l, h])
                sf, kf = load_w(hw_d["f"][l, h])
                si, ki = load_w(hw_d["i"][l, h])
                if full_groups:
                    sg_, kg = load_w(hw_d["g"][l, h])
                lbc = colsT[:, C_LB + 16 * l + h:C_LB + 16 * l + h + 1]
                omc = colsT[:, C_OMLB + 16 * l + h:C_OMLB + 16 * l + h + 1]
                nomc = colsT[:, C_NOMLB + 16 * l + h:C_NOMLB + 16 * l + h + 1]
                goc = colsT[:, C_GO + l:C_GO + l + 1]
                for (c0, n) in groups:
                    _hg_group(l, h, c0, n, sq, kq, sf, kf, si, ki, sg_, kg, lbc, omc, nomc, goc, (c0, n) not in full_groups)
            if last_tile:
                S.op("sp", lambda e: e.dma_start(out=stp_d[l].rearrange("h k v -> k h v"), in_=Sst[:, l, :, :]),
                     reads=["S_%d_%d" % (l, h) for h in range(16)])
            if full_groups:
                out_proj(hw_d["o"][l], full_groups)

        def _hg_group(l, h, c0, n, sq, kq, sf, kf, si, ki, sg_, kg, lbc, omc, nomc, goc, sonly=False):
            if True:
                if True:
                    smp = (c0 >= TP)
                    C = TS if smp else 32
                    nch = n // C
                    mk = scm[:, 512:512 + n] if smp else scm[:, 0:n]
                    q32, sgb, lfb, bb, epb, k32, = (t32[:, i, 0:n] for i in range(6))
                    qt, khb, kdb, vTb = (t16[:, i, 0:n] for i in range(4))
                    if not sonly:
                        proj(pP[:, 0, 0:n], sq, kq, hT, ["hT"], c0, n, "pP0")
                        S.op("act", lambda e: e.activation(out=q32, in_=pP[:, 0, 0:n], func=AF.Silu), reads=["pP0"], writes=["t32_0"])
                    proj(pP[:, 1, 0:n], sf, kf, hT, ["hT"], c0, n, "pP1")
                    S.op("act", lambda e: e.activation(out=sgb, in_=pP[:, 1, 0:n], func=AF.Sigmoid), reads=["pP1"], writes=["t32_1"])
                    S.op("act", lambda e: e.activation(out=lfb, in_=sgb, func=AF.Ln, scale=omc, bias=lbc),
                         reads=["t32_1", "colsT"], writes=["t32_2"])
                    S.op("dve", lambda e: e.tensor_scalar_max(out=lfb, in0=lfb, scalar1=LN_MINF), reads=["t32_2"], writes=["t32_2"])
                    S.op("dve", lambda e: e.tensor_scalar(out=k32, in0=sgb, scalar1=nomc, scalar2=omc, op0=ALU.mult, op1=ALU.add),
                         reads=["t32_1", "colsT"], writes=["t32_5"])
                    S.op("dve", lambda e: e.tensor_tensor_scan(out=bb, data0=mk, data1=lfb, initial=0.0, op0=ALU.mult, op1=ALU.add),
                         reads=["t32_2", "scm"], writes=["t32_3"])
                    S.op("act", lambda e: e.activation(out=epb, in_=bb, func=AF.Exp), reads=["t32_3"], writes=["t32_4"])
                    S.op("act", lambda e: e.activation(out=sgb, in_=bb, func=AF.Exp, scale=-1.0), reads=["t32_3"], writes=["t32_1"])
                    if not sonly:
                        S.op("dve", lambda e: e.scalar_tensor_tensor(out=qt, in0=q32, scalar=QSC, in1=epb, op0=ALU.mult, op1=ALU.mult),
                             reads=["t32_0", "t32_4"], writes=["t16_0"])
                    S.op("dve", lambda e: e.tensor_tensor(out=lfb, in0=k32, in1=sgb, op=ALU.mult), reads=["t32_5", "t32_1"], writes=["t32_2"])
                    if not sonly:
                        S.op("act", lambda e: e.activation(out=khb, in_=lfb, func=AF.Copy), reads=["t32_2"], writes=["t16_1"])
                    ep3 = epb.rearrange("p (c t) -> p c t", t=C)
                    S.op("dve", lambda e: e.tensor_tensor(out=kdb.rearrange("p (c t) -> p c t", t=C),
                                                          in0=lfb.rearrange("p (c t) -> p c t", t=C),
                                                          in1=ep3[:, :, C - 1:C].to_broadcast([128, nch, C]), op=ALU.mult),
                         reads=["t32_2", "t32_4"], writes=["t16_2"])
                    proj(pP[:, 0, 0:n], si, ki, hT, ["hT"], c0, n, "pP0")
                    S.op("act", lambda e: e.activation(out=vTb, in_=pP[:, 0, 0:n], func=AF.Copy), reads=["pP0"], writes=["t16_3"])
                    g32 = t32[:, 3, 0:n]
                    if not sonly:
                        proj(pP[:, 1, 0:n], sg_, kg, hT, ["hT"], c0, n, "pP1")
                        S.op("act", lambda e: e.activation(out=g32, in_=pP[:, 1, 0:n], func=AF.Silu), reads=["pP1"], writes=["t32_3"])
                    def _pe_prep_fns(ci):
                        a0 = ci * C
                        pb = ci % 2
                        ptk = pTb[0:C, pb * 256:pb * 256 + 128]
                        ptv = pTb[0:C, pb * 256 + 128:pb * 256 + 256]
                        pA = pM[0:C, 256 + pb * 32:256 + pb * 32 + C]
                        fl = [lambda e, a0=a0, ptk=ptk: e.transpose(ptk, kdb[:, a0:a0 + C], ident_b),
                              lambda e, a0=a0, ptv=ptv: e.transpose(ptv, vTb[:, a0:a0 + C], ident_b)]
                        if not sonly:
                            fl.append(lambda e, a0=a0, pA=pA: e.matmul(pA, lhsT=khb[:, a0:a0 + C], rhs=qt[:, a0:a0 + C], start=True, stop=True))
                        return fl

                    def _post_prep(ci):
                        pb = ci % 2
                        pA = pM[0:C, 256 + pb * 32:256 + pb * 32 + C]
                        S.op("act", (lambda e, pb=pb: e.activation(out=kvtm[0:C, pb, :, :].rearrange("p a b -> p (a b)"),
                                                                    in_=pTb[0:C, pb * 256:pb * 256 + 256], func=AF.Copy)),
                             reads=["pTb%d" % pb], writes=["kvtm%d" % pb])
                        if not sonly:
                            S.op("dve", (lambda e, pA=pA, pb=pb: e.tensor_tensor(out=A_sb[0:C, pb, 0:C], in0=pA, in1=msk[0:C, 3, 0:C], op=ALU.mult)),
                                 reads=["pA%d" % pb, "msk"], writes=["A_sb%d" % pb])

                    def _iter(ci):
                        a0 = ci * C
                        pb = ci % 2
                        nb = (ci + 1) % 2
                        if smp:
                            s_idx = ci
                            s_in, s_out = Ssmp[:, pb, :], Ssmo[:, pb, :]
                            k_in, k_out = "Ssmp%d" % pb, "Ssmo%d" % pb
                            S.op("sp", (lambda e, s_idx=s_idx, s_in=s_in: e.dma_start(out=s_in, in_=st_in[l, s_idx, h])), writes=[k_in])
                        else:
                            bufs = [(Sst[:, l, h, :], "S_%d_%d" % (l, h)), (Stmp[:, :], "Stmp")]
                            (s_in, k_in), (s_out, k_out) = bufs[ci % 2], bufs[(ci + 1) % 2]
                        sbf = S_bf[:, pb, :]
                        if not sonly:
                            S.op("act", (lambda e, sbf=sbf, s_in=s_in: e.activation(out=sbf, in_=s_in, func=AF.Copy)),
                                 reads=[k_in], writes=["S_bf%d" % pb])
                        pS = pM[:, pb * 128:(pb + 1) * 128]
                        fns = []
                        rd = ["kvtm%d" % pb]
                        wr_ = ["pS%d" % pb]
                        if ci + 1 < nch:
                            fns += _pe_prep_fns(ci + 1)
                            rd += ["t16_0", "t16_1", "t16_2", "t16_3", "cstb"]
                            wr_ += ["pTb%d" % nb, "pA%d" % nb]
                        fns.append(lambda e, pb=pb, pS=pS: e.matmul(pS, lhsT=kvtm[0:C, pb, 0, :], rhs=kvtm[0:C, pb, 1, :], start=True, stop=True))
                        S.pe(fns, reads=rd, writes=wr_)
                        if ci + 1 < nch:
                            _post_prep(ci + 1)
                        if not sonly:
                            S.pe([lambda e, a0=a0, pb=pb: e.matmul(pO[:, a0:a0 + C], lhsT=kvtm[0:C, pb, 1, :], rhs=A_sb[0:C, pb, 0:C],
                                                                   start=True, stop=False),
                                  lambda e, a0=a0, sbf=sbf: e.matmul(pO[:, a0:a0 + C], lhsT=sbf, rhs=qt[:, a0:a0 + C], start=False, stop=True)],
                                 reads=["kvtm%d" % pb, "A_sb%d" % pb, "S_bf%d" % pb, "t16_0"], writes=["pO"])
                        S.op("dve", (lambda e, s_in=s_in, s_out=s_out, pS=pS, a0=a0: e.scalar_tensor_tensor(
                            out=s_out, in0=s_in, scalar=epb[:, a0 + C - 1:a0 + C], in1=pS, op0=ALU.mult, op1=ALU.add)),
                            reads=["pS%d" % pb, "t32_4", k_in], writes=[k_out])
                        if smp:
                            S.op("sp", (lambda e, s_idx=s_idx, s_out=s_out: e.dma_start(out=sts_d[l, s_idx, h], in_=s_out)),
                                 reads=[k_out])
                    S.pe(_pe_prep_fns(0), reads=["t16_0", "t16_1", "t16_2", "t16_3", "cstb"], writes=["pTb0", "pA0"])
                    _post_prep(0)
                    for ci in range(nch):
                        _iter(ci)
                    if sonly:
                        return
                    osq = sqb[:, 0, 0:n]
                    S.op("act", lambda e: e.activation(out=osq, in_=pO[:, 0:n], func=AF.Square), reads=["pO"], writes=["sqb0"])
                    S.pe([lambda e: e.matmul(pN[:, 0:n], lhsT=ones_b, rhs=osq, start=True, stop=True)], reads=["sqb0", "cstb"], writes=["pN"])
                    rstd_from(pN[:, 0:n], n, 1.0 / 128, "pN")
                    S.op("dve", lambda e: e.scalar_tensor_tensor(out=q32, in0=pO[:, 0:n], scalar=goc, in1=rstd[:, 0:n],
                                                                 op0=ALU.mult, op1=ALU.mult),
                         reads=["pO", "rstd", "colsT"], writes=["t32_0"])
                    S.op("dve", lambda e: e.tensor_tensor(out=ogT[:, h * TW + c0:h * TW + c0 + n], in0=q32, in1=g32, op=ALU.mult),
                         reads=["t32_0", "t32_3"], writes=["ogT"])

        def out_proj(w_dram, groups):
            og3 = ogT[:].rearrange("p (a b) -> p a b", b=TW)
            for fo in range(16):
                so, ko = load_w(w_dram[fo])
                for gi, (c0, n) in enumerate(groups):
                    pb = (fo * 3 + gi) % 2
                    proj(pP[:, pb, 0:n], so, ko, og3, ["ogT"], c0, n, "pP%d" % pb)
                    S.op("dve", (lambda e, pb=pb, fo=fo, c0=c0, n=n: e.tensor_tensor(out=xT[:, fo, c0:c0 + n], in0=xT[:, fo, c0:c0 + n],
                                                                                      in1=pP[:, pb, 0:n], op=ALU.add)),
                         reads=["pP%d" % pb, "xT"], writes=["xT"])

        def ffn_layer(l, groups):
            rmsnorm_to_h(C_FN + 16 * l, groups)
            ng = len(groups)
            def hid(fb, gi, j, n):
                o = ((fb * 3 + gi) * 4 + j) * 512
                return ogT[:, o:o + n]
            for fc in range(cfg.get("nfc", NFC)):
                fb = fc % 2
                for j in range(4):
                    f = fc * 4 + j
                    sg_, kg = load_w(wgate_d[l, f])
                    su, ku = load_w(wup_d[l, f])
                    for gi, (c0, n) in enumerate(groups):
                        pb = (j * 3 + gi) % 2
                        proj(pP[:, pb, 0:n], sg_, kg, hT, ["hT"], c0, n, "pP%d" % pb)
                        proj(pU[:, pb, 0:n], su, ku, hT, ["hT"], c0, n, "pU%d" % pb)
                        gs = t32[:, pb, 0:n]
                        S.op("act", (lambda e, gs=gs, pb=pb, n=n: e.activation(out=gs, in_=pP[:, pb, 0:n], func=AF.Silu)),
                             reads=["pP%d" % pb], writes=["t32_%d" % pb])
                        S.op("dve", (lambda e, gs=gs, pb=pb, n=n, fb=fb, gi=gi, j=j: e.tensor_tensor(
                            out=hid(fb, gi, j, n), in0=gs, in1=pU[:, pb, 0:n], op=ALU.mult)),
                            reads=["t32_%d" % pb, "pU%d" % pb], writes=["hid%d_%d" % (fb, gi)])
                sd, kd = load_w(wdown_d[l, fc], nslots=4)
                for fo in range(16):
                    for gi, (c0, n) in enumerate(groups):
                        pb = (fo * 3 + gi) % 2
                        fns = []
                        for j in range(4):
                            fns.append(lambda e, j=j, pb=pb, fo=fo, gi=gi, n=n, sd=sd, fb=fb: e.matmul(
                                pP[:, pb, 0:n], lhsT=wr[:, sd + j, fo * 128:(fo + 1) * 128], rhs=hid(fb, gi, j, n),
                                start=(j == 0), stop=(j == 3)))
                        S.pe(fns, reads=kd + ["hid%d_%d" % (fb, gi)], writes=["pP%d" % pb])
                        S.op("dve", (lambda e, pb=pb, fo=fo, c0=c0, n=n: e.tensor_tensor(out=xT[:, fo, c0:c0 + n], in0=xT[:, fo, c0:c0 + n],
                                                                                          in1=pP[:, pb, 0:n], op=ALU.add)),
                             reads=["pP%d" % pb, "xT"], writes=["xT"])

        def aview(off_words, nwords, dt):
            v = arena[:, off_words:off_words + nwords]
            return v.bitcast(dt) if dt != F32 else v
        KW = 128 + TW
        kT = aview(0, 2368, BF16).rearrange("p (a b) -> p a b", b=KW)
        vaug = aview(2368, 2340, BF16).rearrange("p (a b c) -> p a b c", b=8, c=65)
        vsn = aview(4708, 1040, BF16).rearrange("p (a b c) -> p a b c", b=8, c=65)
        kck = aview(5748, 256, BF16).rearrange("p (a b) -> p a b", b=128)
        vck = aview(6004, 260, BF16).rearrange("p (a b c) -> p a b c", b=2, c=65)
        otm = aview(6264, 64, BF16)
        pTs = aview(6328, 256, BF16).rearrange("p (a b c) -> p a b c", b=2, c=128)
        scs = aview(6584, 512, F32).rearrange("p (a b c) -> p a b c", b=2, c=128)
        qTc = aview(7096, 528, BF16)
        esk = aview(7624, 64, F32)
        skb = aview(7688, 64, F32)
        rcs = aview(7752, 4, F32)

        def rope_apply(src, srckey, n, c0, halo):
            if halo:
                S.op("sp", lambda e: e.dma_start(out=t32[:, 4, 0:n], in_=rope_d[2][:, 0:n]), writes=["t32_4"])
                S.op("sp", lambda e: e.dma_start(out=t32[:, 5, 0:n], in_=rope_d[3][:, 0:n]), writes=["t32_5"])
            else:
                S.op("sp", lambda e: e.dma_start(out=t32[:, 4, 0:n], in_=rope_d[0][:, c0:c0 + n]), writes=["t32_4"])
                S.op("sp", lambda e: e.dma_start(out=t32[:, 5, 0:n], in_=rope_d[1][:, c0:c0 + n]), writes=["t32_5"])
            S.pe([lambda e: e.matmul(pN[:, 0:n], lhsT=rotP, rhs=src, start=True, stop=True)], reads=[srckey, "cst"], writes=["pN"])
            S.op("dve", lambda e: e.tensor_tensor(out=t32[:, 2, 0:n], in0=pN[:, 0:n], in1=t32[:, 5, 0:n], op=ALU.mult),
                 reads=["pN", "t32_5"], writes=["t32_2"])
            S.op("dve", lambda e: e.tensor_tensor(out=t32[:, 3, 0:n], in0=src, in1=t32[:, 4, 0:n], op=ALU.mult),
                 reads=[srckey, "t32_4"], writes=["t32_3"])
            S.op("dve", lambda e: e.tensor_tensor(out=t32[:, 3, 0:n], in0=t32[:, 3, 0:n], in1=t32[:, 2, 0:n], op=ALU.add),
                 reads=["t32_3", "t32_2"], writes=["t32_3"])

        def normed_rot(pin, pkey, n, c0, halo, gcol):
            sq32 = t32[:, 0, 0:n]
            kn = t32[:, 1, 0:n]
            S.op("act", lambda e: e.activation(out=sq32, in_=pin, func=AF.Square), reads=[pkey], writes=["t32_0"])
            S.pe([lambda e: e.matmul(pN[:, 0:n], lhsT=bd64, rhs=sq32, start=True, stop=True)], reads=["t32_0", "cst"], writes=["pN"])
            rstd_from(pN[:, 0:n], n, 1.0 / 64, "pN")
            S.op("dve", lambda e: e.scalar_tensor_tensor(out=kn, in0=pin, scalar=colsT[:, gcol:gcol + 1], in1=rstd[:, 0:n],
                                                         op0=ALU.mult, op1=ALU.mult),
                 reads=[pkey, "rstd", "colsT"], writes=["t32_1"])
            rope_apply(kn, "t32_1", n, c0, halo)

        def kv_compute(groups, halo):
            kvs = cfg.get("kvs", 99)
            rmsnorm_to_h(C_KVN, groups)
            for m in range(4):
                sk, kk = load_w(wk_d[m])
                for (c0, n) in groups:
                    proj(pP[:, 0, 0:n], sk, kk, hT, ["hT"], c0, n, "pP0")
                    if kvs < 2:
                        continue
                    normed_rot(pP[:, 0, 0:n], "pP0", n, c0, halo, C_KN)
                    if kvs < 3:
                        continue
                    if halo:
                        S.op("act", (lambda e, m=m, n=n: e.activation(out=khalo[:, m, :], in_=t32[:, 3, 0:n], func=AF.Copy)),
                             reads=["t32_3"], writes=["khalo"])
                    else:
                        S.op("act", (lambda e, m=m, c0=c0, n=n: e.activation(out=kT[:, m, 128 + c0:128 + c0 + n], in_=t32[:, 3, 0:n], func=AF.Copy)),
                             reads=["t32_3"], writes=["kT"])
                        if kvs < 4:
                            continue
                        if c0 == 512:
                            S.op("sp", (lambda e, m=m: e.dma_start(out=kwp_d[m], in_=t32[:, 3, 384:512])), reads=["t32_3"])
                        if c0 == TP:
                            S.op("sp", (lambda e, m=m, n=n: e.dma_start(out=kws_new[m], in_=t32[:, 3, 0:n])), reads=["t32_3"])
            for fo in range(4 if kvs >= 5 else 0):
                sv, kv = load_w(wv_d[fo])
                for (c0, n) in groups:
                    proj(pP[:, 1, 0:n], sv, kv, hT, ["hT"], c0, n, "pP1")
                    if halo:
                        S.op("act", (lambda e, fo=fo, n=n: e.activation(out=vhalo[:, fo, :], in_=pP[:, 1, 0:n], func=AF.Copy)),
                             reads=["pP1"], writes=["vhalo"])
                        continue
                    v32 = t32[:, 4, 0:n]
                    vb = t16[:, 0, 0:n]
                    S.op("act", (lambda e, v32=v32, n=n: e.activation(out=v32, in_=pP[:, 1, 0:n], func=AF.Copy)), reads=["pP1"], writes=["t32_4"])
                    S.op("act", (lambda e, vb=vb, v32=v32: e.activation(out=vb, in_=v32, func=AF.Copy)), reads=["t32_4"], writes=["t16_0"])
                    if kvs < 6:
                        continue
                    if c0 == 512:
                        S.op("sp", (lambda e, fo=fo: e.dma_start(out=vwp_d[fo], in_=t32[:, 4, 384:512])), reads=["t32_4"])
                    if c0 == TP:
                        S.op("sp", (lambda e, fo=fo, n=n: e.dma_start(out=vws_new[fo], in_=t32[:, 4, 0:n])), reads=["t32_4"])
                    if kvs < 7:
                        continue
                    if c0 == TP:
                        for s in range(NS):
                            S.pe([lambda e, s=s, vb=vb: e.transpose(pTb[0:TS, 0:128], vb[:, s * TS:(s + 1) * TS], ident_b)],
                                 reads=["t16_0", "cstb"], writes=["pTb0"])
                            S.op("act", (lambda e, s=s, fo=fo: e.activation(out=vsn[0:TS, s, 2 * fo:2 * fo + 2, 0:64],
                                                                             in_=pTb[0:TS, 0:128].rearrange("p (a b) -> p a b", b=64), func=AF.Copy)),
                                 reads=["pTb0"], writes=["vsn"])
                    else:
                        for bi in range(n // 128):
                            kb = 1 + (c0 + bi * 128) // 128
                            S.pe([lambda e, bi=bi, vb=vb: e.transpose(pTb[:, 0:128], vb[:, bi * 128:(bi + 1) * 128], ident_b)],
                                 reads=["t16_0", "cstb"], writes=["pTb0"])
                            S.op("act", (lambda e, kb=kb, fo=fo: e.activation(out=vaug[:, kb, 2 * fo:2 * fo + 2, 0:64],
                                                                               in_=pTb[:, 0:128].rearrange("p (a b) -> p a b", b=64), func=AF.Copy)),
                                 reads=["pTb0"], writes=["vaug"])

        def load_cache(m):
            for s in range(NS):
                ctm = t32[:, 0:2, 0:128]
                S.op("sp", (lambda e, s=s: e.dma_start(out=t32[:, 0, 0:128], in_=ck_d[s][:, m * 128:(m + 1) * 128])), writes=["t32_0"])
                S.op("sp", (lambda e, s=s: e.dma_start(out=t32[:, 1, 0:128], in_=cv_d[s][:, m * 128:(m + 1) * 128])), writes=["t32_1"])
                S.op("act", (lambda e, s=s: e.activation(out=vck[:, s, :, 0:64], in_=t32[:, 1, 0:128].rearrange("p (g d) -> p g d", d=64), func=AF.Copy)),
                     reads=["t32_1"], writes=["vck"])
                S.op("act", (lambda e: e.activation(out=t16[:, 1, 0:128], in_=t32[:, 0, 0:128], func=AF.Copy)), reads=["t32_0"], writes=["t16_1"])
                S.pe([lambda e: e.transpose(pTb[:, 256:384], t16[:, 1, 0:128], ident_b)], reads=["t16_1", "cstb"], writes=["pTb1"])
                S.op("act", (lambda e, s=s: e.activation(out=kck[:, s, :], in_=pTb[:, 256:384], func=AF.Copy)), reads=["pTb1"], writes=["kck"])

        def attn_layer(j, groups):
            rmsnorm_to_h(C_AN + 16 * j, groups)
            og3 = ogT[:].rearrange("p (a b) -> p a b", b=TW)
            it = [0]

            def attend(nq, qc0, fo, m, prevK, prevV, curK, curV, nk, mprev, mcur, pkeys):
                for hh in range(2):
                    half = hh * 64
                    g = 2 * m + hh
                    hq = 8 * m + 4 * hh + (fo % 4)
                    pb = it[0] % 2
                    it[0] += 1
                    qsl = qTc[half:half + 64, qc0:qc0 + nq]
                    sc = scs[:, pb]
                    pT_ = pTs[:, pb]
                    pS0 = pU[:, pb, 0:nq]
                    pS1 = pU[0:nk, pb, 128:128 + nq]
                    S.pe([lambda e, half=half, qsl=qsl, pS0=pS0: e.matmul(pS0, lhsT=prevK(half), rhs=qsl, start=True, stop=True),
                          lambda e, half=half, qsl=qsl, pS1=pS1: e.matmul(pS1, lhsT=curK(half), rhs=qsl, start=True, stop=True)],
                         reads=["qTc", "kT"] + pkeys, writes=["pU%d" % pb])
                    S.op("dve", (lambda e, sc=sc, pS0=pS0: e.scalar_tensor_tensor(out=sc[:, 0, 0:nq], in0=pS0, scalar=ASC, in1=mprev[:, 0:nq],
                                                                                   op0=ALU.mult, op1=ALU.add)),
                         reads=["pU%d" % pb, "msk"], writes=["sc%d" % pb])
                    S.op("dve", (lambda e, sc=sc, pS1=pS1: e.scalar_tensor_tensor(out=sc[0:nk, 1, 0:nq], in0=pS1, scalar=ASC, in1=mcur[0:nk, 0:nq],
                                                                                   op0=ALU.mult, op1=ALU.add)),
                         reads=["pU%d" % pb, "msk"], writes=["sc%d" % pb])
                    S.op("act", (lambda e, sc=sc, pT_=pT_: e.activation(out=pT_[:, 0, 0:nq], in_=sc[:, 0, 0:nq], func=AF.Exp)),
                         reads=["sc%d" % pb], writes=["pT%d" % pb])
                    S.op("act", (lambda e, sc=sc, pT_=pT_: e.activation(out=pT_[0:nk, 1, 0:nq], in_=sc[0:nk, 1, 0:nq], func=AF.Exp)),
                         reads=["sc%d" % pb], writes=["pT%d" % pb])
                    po = pO[0:nq, pb * 128:pb * 128 + 65]
                    S.pe([lambda e, g=g, pT_=pT_, po=po: e.matmul(po, lhsT=pT_[:, 0, 0:nq], rhs=prevV(g), start=True, stop=False),
                          lambda e, g=g, pT_=pT_, po=po: e.matmul(po, lhsT=pT_[0:nk, 1, 0:nq], rhs=curV(g), start=False, stop=True)],
                         reads=["pT%d" % pb, "vaug", "vsn"] + pkeys, writes=["pO%d" % pb])
                    rc = rcs[0:nq, pb:pb + 1]
                    S.op("dve", (lambda e, rc=rc, po=po, hq=hq: e.tensor_tensor(out=rc, in0=po[:, 64:65], in1=esk[0:nq, j * 32 + hq:j * 32 + hq + 1],
                                                                                 op=ALU.add)),
                         reads=["pO%d" % pb, "esk"], writes=["rc%d" % pb])
                    S.op("dve", (lambda e, rc=rc: e.reciprocal(out=rc, in_=rc)), reads=["rc%d" % pb], writes=["rc%d" % pb])
                    S.op("dve", (lambda e, rc=rc, po=po, half=half: e.tensor_scalar(out=otm[0:nq, half:half + 64], in0=po[:, 0:64], scalar1=rc,
                                                                                     scalar2=None, op0=ALU.mult)),
                         reads=["pO%d" % pb, "rc%d" % pb], writes=["otm"])
                S.pe([lambda e: e.transpose(pTb[:, 512:512 + nq], otm[0:nq, :], ident_b[0:nq, 0:nq])], reads=["otm", "cstb"], writes=["pTb2"])
                S.op("act", (lambda e: e.activation(out=og3[:, fo, qc0:qc0 + nq], in_=pTb[:, 512:512 + nq], func=AF.Copy)),
                     reads=["pTb2"], writes=["ogT"])

            for m in range(cfg.get("mstart", 0), cfg.get("nm", 4)):
                if cfg.get("astop", 99) >= 5:
                    load_cache(m)
                for i in range(4 if cfg.get("astop", 99) >= 6 else 0):
                    fo = 4 * m + i
                    sq_, kq = load_w(awq_d[j, fo])
                    for (c0, n) in groups:
                        proj(pP[:, 0, 0:n], sq_, kq, hT, ["hT"], c0, n, "pP0")
                        normed_rot(pP[:, 0, 0:n], "pP0", n, c0, False, C_QN + j)
                        S.op("act", (lambda e, c0=c0, n=n: e.activation(out=qTc[:, c0:c0 + n], in_=t32[:, 3, 0:n], func=AF.Copy)),
                             reads=["t32_3"], writes=["qTc"])
                    for qb in range(8 if cfg.get("astop", 99) >= 7 else 0):
                        k0 = qb * 128
                        attend(128, qb * 128, fo, m,
                               lambda half, k0=k0, m=m: kT[half:half + 64, m, k0:k0 + 128],
                               lambda g, qb=qb: vaug[:, qb, g, :],
                               lambda half, k0=k0, m=m: kT[half:half + 64, m, k0 + 128:k0 + 256],
                               lambda g, qb=qb: vaug[:, qb + 1, g, :],
                               128, msk[:, 2, :] if qb == 0 else msk[:, 0, :], msk[:, 1, :], [])
                    for s in range(NS if cfg.get("astop", 99) >= 8 else 0):
                        c1 = 128 + TP + s * TS
                        attend(TS, TP + s * TS, fo, m,
                               lambda half, s=s: kck[half:half + 64, s, :],
                               lambda g, s=s: vck[:, s, g % 2, :],
                               lambda half, c1=c1, m=m: kT[half:half + 64, m, c1:c1 + TS],
                               lambda g, s=s: vsn[0:TS, s, g, :],
                               TS, msk[:, 0, :], msk[:, 1, :], ["kck", "vck"])
            if cfg.get("astop", 99) >= 9:
                out_proj(awo_d[j], groups)

        xs3 = xs_d.rearrange("p (a b) -> p a b", b=NS * TS)
        for t in range(NT - n_tiles, NT):
            last = (t == NT - 1)
            groups = list(groups_main) + ([(TP, NS * TS)] if last else [])
            S.op("sp", (lambda e, t=t: e.dma_start(out=xT[:, :, 0:TP], in_=xin[t].rearrange("p (a b) -> p a b", b=TP))),
                 writes=["xT"], reads=[])
            if last:
                S.op("sp", lambda e: e.dma_start(out=xT[:, :, TP:TW], in_=xs3), writes=["xT"])
            if t == NT - n_tiles:
                S.op("dve", lambda e: e.memset(arena[:, 0:4096], 0.0),
                     writes=["S_%d_%d" % (l, h) for l in range(2) for h in range(16)])
                S.op("dve", lambda e: e.memset(arena[:, 7760:8272], 0.0), writes=["khalo", "vhalo"])
            for l in range(cfg.get("nl", 2)):
                if l == 1 and not last and cfg.get("prune", True):
                    fullg = [groups[-1]] if (do_attn and t == NT - 2) else []
                else:
                    fullg = groups
                if cfg.get("hg", True):
                    hgrn_layer(l, groups, last, fullg)
                if cfg.get("ffn", True) and fullg:
                    ffn_layer(l, fullg)
            if do_attn and t == NT - 2:
                kv_compute([(TP - 128, 128)], True)
        if do_attn:
            last_groups = list(groups_main) + [(TP, NS * TS)]
            S.barrier()
            S.op("sp", lambda e: e.dma_start(out=skb, in_=cols_d[:, C_SK:C_SK + 64]), writes=["skb"])
            S.op("act", lambda e: e.activation(out=esk, in_=skb, func=AF.Exp), reads=["skb"], writes=["esk"])
            S.op("dve", lambda e: e.memset(vaug[:, :, :, 64:65], 1.0), writes=["vaug"])
            S.op("dve", lambda e: e.memset(vsn[:, :, :, 64:65], 1.0), writes=["vsn"])
            S.op("dve", lambda e: e.memset(vck[:, :, :, 64:65], 1.0), writes=["vck"])
            S.op("act", lambda e: e.activation(out=kT[:, :, 0:128], in_=khalo, func=AF.Copy), reads=["khalo"], writes=["kT"])
            for fo in range(4):
                S.pe([lambda e, fo=fo: e.transpose(pTb[:, 0:128], vhalo[:, fo, :], ident_b)], reads=["vhalo", "cstb"], writes=["pTb0"])
                S.op("act", (lambda e, fo=fo: e.activation(out=vaug[:, 0, 2 * fo:2 * fo + 2, 0:64],
                                                            in_=pTb[:, 0:128].rearrange("p (a b) -> p a b", b=64), func=AF.Copy)),
                     reads=["pTb0"], writes=["vaug"])
            if cfg.get("astop", 99) >= 2:
                kv_compute(last_groups, False)
            if cfg.get("astop", 99) >= 3:
                for s in range(NS):
                    S.op("sp", (lambda e, s=s: e.dma_start(out=kws_old[s], in_=ck_d[s, TS:128, :])))
                    S.op("sp", (lambda e, s=s: e.dma_start(out=vws_old[s], in_=cv_d[s, TS:128, :])))
            for j in range(cfg.get("nl", 2) if cfg.get("astop", 99) >= 4 else 0):
                attn_layer(j, last_groups)
                if cfg.get("ffn", True):
                    ffn_layer(2 + j, last_groups)
        S.op("sp", lambda e: e.dma_start(out=yT_d.rearrange("p (a b) -> p a b", b=TW), in_=xT[:]), reads=["xT"])
        S.finish()
        with nc.Block() as block:
            S.emit(block)
    return nc


_NC_CACHE = {}


def _relayout_w(w, ncol_chunks):
    K, F = w.shape
    a = w.reshape(K // 128, 128, F // 128, 128)
    return np.ascontiguousarray(a.transpose(2, 1, 0, 3)).reshape(F // 128, 128, (K // 128) * 128)


def kernel(x_prompt, x_sample, state_hgrn, cache_k_win, cache_v_win,
           hgrn_norm, hgrn_wq, hgrn_wf, hgrn_wi, hgrn_wg, hgrn_lb_logits, hgrn_onorm, hgrn_wo,
           kv_norm, w_k, w_v, k_norm,
           attn_norm, attn_wq, q_norm, sinks, attn_wo,
           ffn_norm, w_gate, w_up, w_down, _cfg=None):
    f = lambda a: np.asarray(a, dtype=np.float32)
    x_prompt, x_sample, state_hgrn = f(x_prompt), f(x_sample), f(state_hgrn)
    cache_k_win, cache_v_win = f(cache_k_win), f(cache_v_win)
    ncores = 8 if (_cfg or {}).get("cores") else (_cfg or {}).get("ncores", 8)
    shared = {}
    for nm, w in (("q", hgrn_wq), ("f", hgrn_wf), ("i", hgrn_wi), ("g", hgrn_wg), ("o", hgrn_wo)):
        w = f(w)
        shared["hw_" + nm] = np.stack([_relayout_w(w[l], 16) for l in range(2)])
    shared["wgate"] = np.stack([_relayout_w(f(w_gate)[l], NF) for l in range(4)])
    shared["wup"] = np.stack([_relayout_w(f(w_up)[l], NF) for l in range(4)])
    wd = f(w_down)
    shared["wdown"] = np.ascontiguousarray(wd.reshape(4, NFC, 4, 128, 2048))
    perm = []
    for fo in range(16):
        m, i = fo // 4, fo % 4
        for hq in (8 * m + i, 8 * m + 4 + i):
            perm.extend(range(hq * 64, hq * 64 + 64))
    perm = np.array(perm)
    awq = f(attn_wq)[:, :, perm]
    awo = f(attn_wo)[:, perm, :]
    shared["awq"] = np.stack([_relayout_w(awq[j], 16) for j in range(2)])
    shared["awo"] = np.stack([_relayout_w(awo[j], 16) for j in range(2)])
    shared["wk"] = _relayout_w(f(w_k), 4)
    shared["wv"] = _relayout_w(f(w_v), 4)

    def colify(v):
        return f(v).reshape(16, 128).T

    cols = np.zeros((128, 320), np.float32)
    hn, an, fn_ = f(hgrn_norm), f(attn_norm), f(ffn_norm)
    for l in range(2):
        cols[:, 0 + 16 * l:16 + 16 * l] = colify(hn[l])
        cols[:, 32 + 16 * l:48 + 16 * l] = colify(an[l])
    for l in range(4):
        cols[:, 64 + 16 * l:80 + 16 * l] = colify(fn_[l])
    cols[:, 128:144] = colify(kv_norm)
    cols[:, 240:242] = f(hgrn_onorm).T
    cols[:, 242] = np.tile(f(k_norm), 2)
    cols[:, 243:245] = np.tile(f(q_norm), (1, 2)).T
    cols[:, 256:320] = np.broadcast_to(f(sinks).reshape(1, 64), (128, 64))
    shared["cols_base"] = cols
    shared["lb_logits"] = np.ascontiguousarray(f(hgrn_lb_logits).reshape(2, 16, 128).transpose(2, 0, 1)).reshape(128, 32)

    cst = np.zeros((4, 128, 128), np.float32)
    cst[0] = np.eye(128)
    cst[1] = 1.0
    cst[2, :64, :64] = 1.0
    cst[2, 64:, 64:] = 1.0
    for mm in range(128):
        b, o = (mm // 64) * 64, mm % 64
        if o < 32:
            cst[3, b + o + 32, mm] = -1.0
        else:
            cst[3, b + o - 32, mm] = 1.0
    kk = np.arange(128)[:, None]
    qq = np.arange(128)[None, :]
    masks = np.zeros((4, 128, 128), np.float32)
    masks[0] = np.where(kk > qq, 0.0, NEG)
    masks[1] = np.where(kk <= qq, 0.0, NEG)
    masks[3, :32, :32] = (np.arange(32)[:, None] <= np.arange(32)[None, :]).astype(np.float32)
    scm = np.ones((128, 544), np.float32)
    scm[:, 0:512:32] = 0.0
    scm[:, 512:544:8] = 0.0
    half = 32
    inv = (10000.0 ** (-np.arange(half, dtype=np.float32) / half)).astype(np.float32)
    invp = inv[np.arange(128) % 32][:, None]

    in_maps = []
    for c in range(ncores):
        seq, r = c // 4, c % 4
        nreal = (r + 1) * TP
        xp = np.zeros((SEQ, D), np.float32)
        xp[SEQ - nreal:] = x_prompt[seq, :nreal]
        xin = np.ascontiguousarray(xp.reshape(NT, TP, 16, 128).transpose(0, 3, 2, 1)).reshape(NT, 128, 16 * TP)
        xsm = x_sample[NS * c:NS * (c + 1)].reshape(NS * TS, 16, 128)
        xs = np.ascontiguousarray(xsm.transpose(2, 1, 0)).reshape(128, 16 * NS * TS)
        pos = np.concatenate([r * TP + np.arange(TP), np.tile(PAST + np.arange(TS), NS)]).astype(np.float32)
        posh = (r * TP - 128 + np.arange(128)).astype(np.float32)
        rope = np.zeros((4, 128, TW), np.float32)
        ang = invp * pos[None, :]
        rope[0], rope[1] = np.cos(ang), np.sin(ang)
        angh = invp * posh[None, :]
        rope[2, :, :128], rope[3, :, :128] = np.cos(angh), np.sin(angh)
        mk = masks.copy()
        mk[2] = masks[0] if r > 0 else NEG
        m = dict(shared)
        m.pop("cols_base"); m.pop("lb_logits")
        m.update(xin=xin, xs=xs,
                 st_in=np.ascontiguousarray(state_hgrn[:, NS * c:NS * (c + 1)]),
                 ck=np.ascontiguousarray(cache_k_win[NS * c:NS * (c + 1)].reshape(NS, 128, 512)),
                 cv=np.ascontiguousarray(cache_v_win[NS * c:NS * (c + 1)].reshape(NS, 128, 512)),
                 cols=cols, lbl=shared["lb_logits"], rope=rope, masks=mk, cst=cst, scm=scm)
        in_maps.append(m)

    key = repr(_cfg)
    if key not in _NC_CACHE:
        _NC_CACHE[key] = build_nc(_cfg)
    nc = _NC_CACHE[key]
    sel = (_cfg or {}).get("cores")
    if sel:
        in_maps = [in_maps[c] for c in sel]
    res = run_bass_kernel_spmd(nc, in_maps, core_ids=list(range(len(in_maps))), trace=bool((_cfg or {}).get("trace", False)))
    if sel:
        full = [None] * 8
        for i, c in enumerate(sel):
            full[c] = res.results[i]
        return full
    if (_cfg or {}).get("trace"):
        print("EXEC_TIME_NS", res.exec_time_ns)
    R = res.results
    global _LAST_R
    _LAST_R = R
    y_prompt = np.zeros((2, SEQ, D), np.float32)
    y_sample = np.zeros((32, TS, D), np.float32)
    st_p = np.zeros((2, 2, 16, 128, 128), np.float32)
    st_s = np.zeros((2, 32, 16, 128, 128), np.float32)
    kw_p = np.zeros((2, 128, 8, 64), np.float32)
    vw_p = np.zeros((2, 128, 8, 64), np.float32)
    kw_s = np.zeros((32, 128, 8, 64), np.float32)
    vw_s = np.zeros((32, 128, 8, 64), np.float32)
    for c in range(ncores):
        seq, r = c // 4, c % 4
        yT = R[c]["yT"].reshape(128, 16, TW)
        ytm = yT.transpose(2, 1, 0).reshape(TW, D)
        y_prompt[seq, r * TP:(r + 1) * TP] = ytm[:TP]
        y_sample[NS * c:NS * (c + 1)] = ytm[TP:].reshape(NS, TS, D)
        st_s[:, NS * c:NS * (c + 1)] = R[c]["sts"]
        knew = R[c]["kws_new"].reshape(512, NS, TS).transpose(1, 2, 0)
        vnew = R[c]["vws_new"].reshape(512, NS, TS).transpose(1, 2, 0)
        kw_s[NS * c:NS * (c + 1), :120] = R[c]["kws_old"].reshape(NS, 120, 8, 64)
        vw_s[NS * c:NS * (c + 1), :120] = R[c]["vws_old"].reshape(NS, 120, 8, 64)
        kw_s[NS * c:NS * (c + 1), 120:] = knew.reshape(NS, TS, 8, 64)
        vw_s[NS * c:NS * (c + 1), 120:] = vnew.reshape(NS, TS, 8, 64)
        if r == 3:
            st_p[:, seq] = R[c]["stp"]
            kw_p[seq] = R[c]["kwp"].reshape(512, 128).T.reshape(128, 8, 64)
            vw_p[seq] = R[c]["vwp"].reshape(512, 128).T.reshape(128, 8, 64)
    return (y_prompt, y_sample, st_p, st_s, kw_p, vw_p, kw_s, vw_s)
```
